# Optimizing a Trainium2 kernel written in Bass

```python
import math
import jax, jax.numpy as jnp
from jax import lax
import numpy as np


D_MODEL = 4096
BATCH = 8
SEQ = 2048
DEPTH = 4

HEAD_DIM = 128
A_WIDTH = D_MODEL // 2
B_WIDTH = D_MODEL - A_WIDTH
DIFF_HEADS = A_WIDTH // (2 * HEAD_DIM)
DIL_HEADS = B_WIDTH // HEAD_DIM
DIL_PATTERNS = ((128, 1), (512, 4), (2048, 16))
EVEN_IN = 3 * A_WIDTH + 3 * B_WIDTH
WIN_Q_HEADS = D_MODEL // HEAD_DIM
WIN_KV_HEADS = 8
WIN_HALF = 128
ODD_IN = (WIN_Q_HEADS + 2 * WIN_KV_HEADS) * HEAD_DIM
FFN_HIDDEN = -(-8 * D_MODEL // (3 * 256)) * 256
Q_BLOCK = 128
ROPE_THETA = 10000.0
EPS = 1e-6
NEG_INF = -1e30
N_EVEN = (DEPTH + 1) // 2
N_ODD = DEPTH // 2

kernel_name = 'hybrid_diff_dilated_window_encoder'


def rms_norm(x, g):
    xf = x.astype(jnp.float32)
    y = xf * lax.rsqrt(jnp.mean(xf * xf, axis=-1, keepdims=True) + EPS)
    return (y * g.astype(jnp.float32)).astype(x.dtype)


def rope(x, pos):
    d = x.shape[-1]
    inv_freq = ROPE_THETA ** (-jnp.arange(0, d, 2, dtype=jnp.float32) / d)
    ang = pos.astype(jnp.float32)[:, None] * inv_freq[None, :]
    ang = jnp.concatenate([ang, ang], axis=-1)
    bshape = (1, x.shape[1]) + (1,) * (x.ndim - 3) + (d,)
    cos = jnp.cos(ang).reshape(bshape)
    sin = jnp.sin(ang).reshape(bshape)
    xf = x.astype(jnp.float32)
    rot = jnp.concatenate([-xf[..., d // 2:], xf[..., :d // 2]], axis=-1)
    return (xf * cos + rot * sin).astype(x.dtype)


def banded_attention(q, k, v, half_width, sink=None):
    b, n, h, d = q.shape
    hk = k.shape[2]
    g = h // hk
    blk = half_width
    nb = -(-n // blk)
    n_pad = nb * blk - n
    qp = jnp.pad(q, ((0, 0), (0, n_pad), (0, 0), (0, 0))).reshape(b, nb, blk, hk, g, d)
    kv_pad = ((0, 0), (blk, n_pad + blk), (0, 0), (0, 0))
    kb = jnp.pad(k, kv_pad).reshape(b, nb + 2, blk, hk, d)
    vb = jnp.pad(v, kv_pad).reshape(b, nb + 2, blk, hk, d)
    kw = jnp.concatenate([kb[:, :-2], kb[:, 1:-1], kb[:, 2:]], axis=2)
    vw = jnp.concatenate([vb[:, :-2], vb[:, 1:-1], vb[:, 2:]], axis=2)
    s = jnp.einsum('bnqhgd,bnkhd->bnhgqk', qp, kw, preferred_element_type=jnp.float32) * d ** -0.5
    qpos = jnp.arange(nb)[:, None] * blk + jnp.arange(blk)[None, :]
    kpos = jnp.arange(nb)[:, None] * blk - blk + jnp.arange(3 * blk)[None, :]
    rel = kpos[:, None, :] - qpos[:, :, None]
    valid = (jnp.abs(rel) <= half_width) & (kpos[:, None, :] >= 0) & (kpos[:, None, :] < n)
    s = jnp.where(valid[None, :, None, None], s, NEG_INF)
    m = jnp.max(s, axis=-1)
    if sink is not None:
        sk = sink.astype(jnp.float32).reshape(1, 1, hk, g, 1)
        m = jnp.maximum(m, sk)
        denom = jnp.sum(jnp.exp(s - m[..., None]), axis=-1) + jnp.exp(sk - m)
    else:
        denom = jnp.sum(jnp.exp(s - m[..., None]), axis=-1)
    lse = m + jnp.log(denom)
    p = jnp.exp(s - lse[..., None]).astype(v.dtype)
    o = jnp.einsum('bnhgqk,bnkhd->bnqhgd', p, vw).reshape(b, nb * blk, h, d)[:, :n]
    lse = lse.transpose(0, 1, 4, 2, 3).reshape(b, nb * blk, h)[:, :n]
    return o, lse


def diff_attention(q, k, v, lam):
    b, s_len, h, _, d = q.shape
    nq = s_len // Q_BLOCK
    qb = q.reshape(b, nq, Q_BLOCK, h, 2, d).transpose(1, 0, 2, 3, 4, 5)
    scale = d ** -0.5

    def one_block(qblk):
        s = jnp.einsum('bqhtd,bkhtd->bhtqk', qblk, k, preferred_element_type=jnp.float32) * scale
        p = jax.nn.softmax(s, axis=-1)
        a = p[:, :, 0] - lam * p[:, :, 1]
        return jnp.einsum('bhqk,bkhe->bqhe', a.astype(v.dtype), v)

    out = lax.map(one_block, qb)
    return out.transpose(1, 0, 2, 3, 4).reshape(b, s_len, h, 2 * d)


def dilated_attention(q, k, v):
    b, s_len, h, d = q.shape
    outs, lses = [], []
    for window, dil in DIL_PATTERNS:
        sub = s_len // dil

        def fold(t):
            return t.reshape(b, sub, dil, h, d).transpose(0, 2, 1, 3, 4).reshape(b * dil, sub, h, d)

        o, lse = banded_attention(fold(q), fold(k), fold(v), window // (2 * dil))
        outs.append(o.reshape(b, dil, sub, h, d).transpose(0, 2, 1, 3, 4).reshape(b, s_len, h, d))
        lses.append(lse.reshape(b, dil, sub, h).transpose(0, 2, 1, 3).reshape(b, s_len, h))
    w = jax.nn.softmax(jnp.stack(lses), axis=0)
    return jnp.einsum('pbsh,pbshd->bshd', w.astype(q.dtype), jnp.stack(outs))


def even_mixer(h, pos, layer, w_in, w_out, q_norm_a, k_norm_a, lambdas, subln, q_norm_b, k_norm_b):
    b, s_len, _ = h.shape
    proj = h @ w_in
    aq, ak, av, bq, bk, bv = jnp.split(
        proj, [A_WIDTH, 2 * A_WIDTH, 3 * A_WIDTH, 3 * A_WIDTH + B_WIDTH, 3 * A_WIDTH + 2 * B_WIDTH], axis=-1)
    aq = rope(rms_norm(aq.reshape(b, s_len, DIFF_HEADS, 2, HEAD_DIM), q_norm_a), pos)
    ak = rope(rms_norm(ak.reshape(b, s_len, DIFF_HEADS, 2, HEAD_DIM), k_norm_a), pos)
    av = av.reshape(b, s_len, DIFF_HEADS, 2 * HEAD_DIM)
    lam_init = 0.8 - 0.6 * math.exp(-0.3 * layer)
    lf = lambdas.astype(jnp.float32)
    lam = jnp.exp(jnp.sum(lf[0] * lf[1])) - jnp.exp(jnp.sum(lf[2] * lf[3])) + lam_init
    ao = diff_attention(aq, ak, av, lam)
    ao = rms_norm(ao, subln) * (1.0 - lam_init)
    bq = rope(rms_norm(bq.reshape(b, s_len, DIL_HEADS, HEAD_DIM), q_norm_b), pos)
    bk = rope(rms_norm(bk.reshape(b, s_len, DIL_HEADS, HEAD_DIM), k_norm_b), pos)
    bv = bv.reshape(b, s_len, DIL_HEADS, HEAD_DIM)
    bo = dilated_attention(bq, bk, bv)
    cat = jnp.concatenate([ao.reshape(b, s_len, A_WIDTH), bo.reshape(b, s_len, B_WIDTH)], axis=-1)
    return cat @ w_out


def odd_mixer(h, pos, w_in, w_out, q_norm, k_norm, sink):
    b, s_len, _ = h.shape
    proj = h @ w_in
    qd = WIN_Q_HEADS * HEAD_DIM
    kd = WIN_KV_HEADS * HEAD_DIM
    q, k, v = jnp.split(proj, [qd, qd + kd], axis=-1)
    q = rope(rms_norm(q.reshape(b, s_len, WIN_Q_HEADS, HEAD_DIM), q_norm), pos)
    k = rope(rms_norm(k.reshape(b, s_len, WIN_KV_HEADS, HEAD_DIM), k_norm), pos)
    v = v.reshape(b, s_len, WIN_KV_HEADS, HEAD_DIM)
    o, _ = banded_attention(q, k, v, WIN_HALF, sink=sink)
    return o.reshape(b, s_len, qd) @ w_out


def swiglu(h, w_gate, w_up, w_down):
    return (jax.nn.silu(h @ w_gate) * (h @ w_up)) @ w_down


def setup_inputs(seed: int = 0) -> dict:
    key = jax.random.key(seed)
    ks = jax.random.split(key, 20)
    f32 = jnp.float32

    def dense(k, shape, fan_in):
        return jax.random.normal(k, shape, f32) * fan_in ** -0.5

    def gain(k, shape):
        return 1.0 + 0.02 * jax.random.normal(k, shape, f32)

    return {
        'x': jax.random.normal(ks[0], (BATCH, SEQ, D_MODEL), f32),
        'mix_norm': gain(ks[1], (DEPTH, D_MODEL)),
        'ffn_norm': gain(ks[2], (DEPTH, D_MODEL)),
        'w_gate': dense(ks[3], (DEPTH, D_MODEL, FFN_HIDDEN), D_MODEL),
        'w_up': dense(ks[4], (DEPTH, D_MODEL, FFN_HIDDEN), D_MODEL),
        'w_down': dense(ks[5], (DEPTH, FFN_HIDDEN, D_MODEL), FFN_HIDDEN),
        'hy_w_in': dense(ks[6], (N_EVEN, D_MODEL, EVEN_IN), D_MODEL),
        'hy_w_out': dense(ks[7], (N_EVEN, D_MODEL, D_MODEL), D_MODEL),
        'diff_q_norm': gain(ks[8], (N_EVEN, HEAD_DIM)),
        'diff_k_norm': gain(ks[9], (N_EVEN, HEAD_DIM)),
        'diff_lambda': 0.1 * jax.random.normal(ks[10], (N_EVEN, 4, HEAD_DIM), f32),
        'diff_subln': gain(ks[11], (N_EVEN, 2 * HEAD_DIM)),
        'dil_q_norm': gain(ks[12], (N_EVEN, HEAD_DIM)),
        'dil_k_norm': gain(ks[13], (N_EVEN, HEAD_DIM)),
        'win_w_in': dense(ks[14], (N_ODD, D_MODEL, ODD_IN), D_MODEL),
        'win_w_out': dense(ks[15], (N_ODD, D_MODEL, D_MODEL), D_MODEL),
        'win_q_norm': gain(ks[16], (N_ODD, HEAD_DIM)),
        'win_k_norm': gain(ks[17], (N_ODD, HEAD_DIM)),
        'win_sink': jax.random.normal(ks[18], (N_ODD, WIN_Q_HEADS), f32),
    }


def reference(x, mix_norm, ffn_norm, w_gate, w_up, w_down, hy_w_in, hy_w_out, diff_q_norm, diff_k_norm,
              diff_lambda, diff_subln, dil_q_norm, dil_k_norm, win_w_in, win_w_out, win_q_norm, win_k_norm,
              win_sink):
    pos = jnp.arange(x.shape[1])
    for layer in range(DEPTH):
        h = rms_norm(x, mix_norm[layer])
        if layer % 2 == 0:
            e = layer // 2
            mix = even_mixer(h, pos, layer, hy_w_in[e], hy_w_out[e], diff_q_norm[e], diff_k_norm[e],
                             diff_lambda[e], diff_subln[e], dil_q_norm[e], dil_k_norm[e])
        else:
            o = layer // 2
            mix = odd_mixer(h, pos, win_w_in[o], win_w_out[o], win_q_norm[o], win_k_norm[o], win_sink[o])
        x = x + mix
        h = rms_norm(x, ffn_norm[layer])
        x = x + swiglu(h, w_gate[layer], w_up[layer], w_down[layer])
    return x
```

```python
import math
import numpy as np
import ml_dtypes
import concourse.bass as bass
import concourse.mybir as mybir
from concourse.bass_utils import run_bass_kernel_spmd

F32 = mybir.dt.float32
BF16 = mybir.dt.bfloat16
AF = mybir.ActivationFunctionType
ALU = mybir.AluOpType
AX = mybir.AxisListType

D = 4096
S = 2048
NT = S // 128
HID = 11008
EVEN_IN = 12288
ODD_IN = 6144
EPS = 1e-6
SCALE = 128 ** -0.5
ENGS = ("pe", "act", "dve", "pool", "sp")
SAME_ENGINE_SYNC = True


class Op:
    __slots__ = ("eng", "fn", "deps", "dsem", "need_inc", "sem", "val")


class Prog:
    def __init__(self, nc):
        self.nc = nc
        self.ops = {e: [] for e in ENGS}
        self.res = {}
        self.dma_latest = {}
        self.esem = {}
        self.dsem_h = {}

    def emit(self, eng, fn, reads=(), writes=(), dsem=None, partial=False, extra_deps=()):
        op = Op()
        op.eng = eng
        op.fn = fn
        op.dsem = dsem
        op.need_inc = dsem is not None
        op.sem = None
        op.val = 0
        deps = set(extra_deps)
        key = dsem if dsem is not None else eng
        for r in reads:
            st = self.res.get(r)
            if st is None:
                st = self.res[r] = [{}, {}, {}]
            deps.update(st[1].values())
            st[2][key] = op
        for w in writes:
            st = self.res.get(w)
            if st is None:
                st = self.res[w] = [{}, {}, {}]
            if st[2]:
                st[0] = st[2]
                st[2] = {}
                st[1] = {}
            deps.update(st[0].values())
            if not partial:
                deps.update(st[1].values())
            st[1][key] = op
        fdeps = []
        for d in deps:
            if d is op:
                continue
            if d.dsem is None and d.eng == eng:
                if eng == "pe" or not SAME_ENGINE_SYNC:
                    continue
            d.need_inc = True
            fdeps.append(d)
        op.deps = fdeps
        self.ops[eng].append(op)
        if dsem is not None:
            self.dma_latest[dsem] = op
        return op

    def barrier(self, tiny):
        bs = {}
        for e in ENGS:
            w = [("ps", 7)] if e == "pe" else ()
            bs[e] = self.emit(e, tiny[e][0], writes=w, extra_deps=list(self.dma_latest.values()),
                              dsem=("bar_sp",) if e == "sp" else None)
            bs[e].need_inc = True
        for e in ENGS:
            self.emit(e, tiny[e][1], extra_deps=[bs[x] for x in ENGS if x != e],
                      dsem=("bar_sp2",) if e == "sp" else None)
        self.res = {}
        self.dma_latest = {("bar_sp2",): self.dma_latest[("bar_sp2",)]}

    def finalize(self):
        nc = self.nc
        LIM = 30000
        cnt = {}
        cur = {}
        for e in ENGS:
            for op in self.ops[e]:
                if not op.need_inc:
                    continue
                key = op.dsem if op.dsem is not None else ("eng", e)
                step = 16 if op.dsem is not None else 1
                if key not in cur or cnt[key] + step > LIM:
                    cur[key] = nc.alloc_semaphore("s%d" % len(self.esem))
                    self.esem[len(self.esem)] = cur[key]
                    cnt[key] = 0
                cnt[key] += step
                op.sem = cur[key]
                op.val = cnt[key]

    def replay(self, eng, engine):
        seen = {}
        for op in self.ops[eng]:
            need = {}
            for d in op.deps:
                k = id(d.sem)
                if seen.get(k, 0) >= d.val:
                    continue
                if k not in need or need[k][1] < d.val:
                    need[k] = (d.sem, d.val)
            for k, (sem, val) in need.items():
                engine.wait_ge(sem, val)
                seen[k] = val
            ins = op.fn(engine)
            if op.need_inc:
                ins.then_inc(op.sem, 16 if op.dsem is not None else 1)


class Arena:
    def __init__(self, handle, n):
        self.h = handle
        self.n = n
        self.base = 0
        self.off = 0

    def take(self, nelem_bf16, dtype=BF16):
        o = (self.off + 15) // 16 * 16
        assert o + nelem_bf16 <= self.n, ("arena overflow", o, nelem_bf16, self.n)
        self.off = o + nelem_bf16
        v = self.h[:, o:o + nelem_bf16]
        if dtype == F32:
            v = v.bitcast(F32)
        return v

    def mark_persistent(self):
        self.base = self.off

    def reset(self):
        self.off = self.base


def build(n_layers=4, stop_after=None, debug=False, decl=None, start_at=None):
    nc = bass.Bass("TRN2", target_bir_lowering=False)
    dt = nc.dram_tensor

    decl = decl or {}

    def inp(name, shape, dtype=F32):
        shape = list(shape)
        if name in decl:
            if decl[name] == 0:
                shape = [1, 128, 128]
            else:
                shape[0] = decl[name]
        return dt(name, shape, dtype, kind="ExternalInput").ap()

    x_in = inp("x", [S, D])
    mix_norm = inp("mix_norm", [4, D])
    ffn_norm = inp("ffn_norm", [4, D])
    w_gate = inp("w_gate", [4, D, HID])
    w_up = inp("w_up", [4, D, HID])
    w_down = inp("w_down", [4, HID, D])
    hy_w_in = inp("hy_w_in", [2, D, EVEN_IN])
    hy_w_out = inp("hy_w_out", [2, D, D])
    diff_q_norm = inp("diff_q_norm", [2, 128])
    diff_k_norm = inp("diff_k_norm", [2, 128])
    diff_lambda = inp("diff_lambda", [2, 512])
    diff_subln = inp("diff_subln", [2, 256])
    dil_q_norm = inp("dil_q_norm", [2, 128])
    dil_k_norm = inp("dil_k_norm", [2, 128])
    win_w_in = inp("win_w_in", [2, D, ODD_IN])
    win_w_out = inp("win_w_out", [2, D, D])
    win_q_norm = inp("win_q_norm", [2, 128])
    win_k_norm = inp("win_k_norm", [2, 128])
    win_sink = inp("win_sink", [2, 32])
    c_ident = inp("c_ident", [128, 128], BF16)
    c_cos = inp("c_cos", [S, 128])
    c_sin = inp("c_sin", [S, 128])
    c_mlong = inp("c_mlong", [128, 3968], BF16)
    c_tri = inp("c_tri", [128, 256], BF16)
    y = dt("y", [S, D], F32, kind="ExternalOutput").ap()
    sk = dict(kind="ExternalOutput") if debug else {}
    hT_d = dt("hT_d", [D, S], BF16, **sk).ap()
    qkv_d = dt("qkv_d", [S, EVEN_IN], BF16, **sk).ap()
    catT_d = dt("catT_d", [D, S], BF16, **sk).ap()
    actT_d = dt("actT_d", [HID, S], BF16, **sk).ap()
    bar_d = dt("bar_d", [2, 128], F32).ap()

    NAR = 105600
    arena_h = nc.alloc_sbuf_tensor("arena", [128, NAR], BF16)
    A = Arena(arena_h, NAR)
    PS = [nc.alloc_psum_tensor("ps%d" % i, [128, 512], F32) for i in range(8)]

    def psb(i):
        return PS[i][:, :].bitcast(BF16).rearrange("p (a b) -> p a b", a=8)

    P = Prog(nc)
    ident = A.take(128)
    bsc = A.take(64, F32)
    A.mark_persistent()

    def dma(eng, out, in_, reads, writes, dsem, partial=False, slow=False):
        if slow:
            return P.emit(eng, lambda e: e.dma_start(out=out, in_=in_, allow_slow_non_contiguous=True), reads, writes,
                          dsem=dsem, partial=partial)
        return P.emit(eng, lambda e: e.dma_start(out=out, in_=in_), reads, writes, dsem=dsem, partial=partial)

    def mm(out, lhsT, rhs, start, stop, reads, writes):
        return P.emit("pe", lambda e: e.matmul(out, lhsT, rhs, start=start, stop=stop), reads, writes)

    def tr(out, in_, reads, writes):
        return P.emit("pe", lambda e: e.transpose(out, in_, ident), reads, writes)

    def act(out, in_, func, reads, writes, scale=None, bias=None, accum=None):
        def f(e):
            kw = {}
            if scale is not None:
                kw["scale"] = scale
            if bias is not None:
                kw["bias"] = bias
            if accum is not None:
                kw["accum_out"] = accum
            return e.activation(out, in_, func, **kw)
        return P.emit("act", f, reads, writes)

    def tt(eng, out, in0, in1, op, reads, writes):
        return P.emit(eng, lambda e: e.tensor_tensor(out, in0, in1, op), reads, writes)

    def ts(eng, out, in0, s1, op0, reads, writes, s2=None, op1=None):
        if op1 is None:
            return P.emit(eng, lambda e: e.tensor_scalar(out, in0, s1, None, op0), reads, writes)
        return P.emit(eng, lambda e: e.tensor_scalar(out, in0, s1, s2, op0, op1), reads, writes)

    def stt(out, in0, scalar, in1, op0, op1, reads, writes):
        return P.emit("dve", lambda e: e.scalar_tensor_tensor(out, in0, scalar, in1, op0, op1), reads, writes)

    def recip(out, in_, reads, writes):
        return P.emit("dve", lambda e: e.reciprocal(out, in_), reads, writes)

    def tred(out, in_, reads, writes):
        return P.emit("dve", lambda e: e.tensor_reduce(out, in_, AX.X, ALU.add), reads, writes)

    def cp(eng, out, in_, reads, writes):
        if eng == "act":
            return act(out, in_, AF.Copy, reads, writes)
        return P.emit(eng, lambda e: e.tensor_copy(out, in_), reads, writes)

    tiny = {
        "pe": (lambda e: e.matmul(PS[7][:, 0:128], ident, ident, start=True, stop=True),
               lambda e: e.matmul(PS[7][:, 128:256], ident, ident, start=True, stop=True)),
        "act": (lambda e: e.activation(bsc[:, 0:2], bsc[:, 2:4], AF.Copy),
                lambda e: e.activation(bsc[:, 4:6], bsc[:, 6:8], AF.Copy)),
        "dve": (lambda e: e.memset(bsc[:, 8:10], 0.0), lambda e: e.memset(bsc[:, 10:12], 0.0)),
        "pool": (lambda e: e.memset(bsc[:, 12:14], 0.0), lambda e: e.memset(bsc[:, 14:16], 0.0)),
        "sp": (lambda e: e.dma_start(out=bar_d[0:1, :], in_=c_cos[0:1, :]),
               lambda e: e.dma_start(out=bar_d[1:2, :], in_=c_cos[1:2, :])),
    }

    def phase_end():
        P.barrier(tiny)
        A.reset()

    dma("sp", ident, c_ident[:, :], [], ["ident"], ("c", 0))
    P.emit("dve", lambda e: e.memset(bsc[:, :], 0.0), [], ["bsc"])
    phase_end()

    def phase_norm(src, gain_row):
        xt = [A.take(8192, F32) for _ in range(4)]
        hb = [A.take(4096) for _ in range(4)]
        hTs = [A.take(32 * 512).rearrange("p (k t) -> p k t", k=32) for _ in range(2)]
        gcol = A.take(64, F32)
        ssb = A.take(64, F32)
        gsrc = gain_row.rearrange("(k p) -> p k", p=128)
        for kq in range(4):
            dma("sp", gcol[:, kq * 8:(kq + 1) * 8], gsrc[:, kq * 8:(kq + 1) * 8], [], ["gcol"], ("c", 1), partial=True, slow=True)
        for t in range(NT):
            s = t % 4
            g4 = t // 4
            sl = g4 % 2
            dma("sp", xt[s], src[t * 128:(t + 1) * 128, :], [("x", t, c) for c in range(8)], [("xt", s)], ("xt", s))
            ss = ssb[:, 4 * s:4 * s + 1]
            sd = ssb[:, 4 * s + 1:4 * s + 2]
            rs = ssb[:, 4 * s + 2:4 * s + 3]
            act(hb[s], xt[s], AF.Square, [("xt", s)], [("hb", s), ("ss", s)], accum=ss)
            act(sd, ss, AF.Sqrt, [("ss", s)], [("sd", s)], scale=1.0 / D, bias=EPS)
            recip(rs, sd, [("sd", s)], [("rs", s)])
            if t % 2 == 0:
                act(hb[s], xt[s], AF.Copy, [("xt", s), ("rs", s)], [("hb", s)], scale=rs)
            else:
                ts("dve", hb[s], xt[s], rs, ALU.mult, [("xt", s), ("rs", s)], [("hb", s)])
            for kg in range(4):
                b = kg % 2
                tp = psb(b)
                for k in range(8):
                    kc = kg * 8 + k
                    tr(tp[:, k, :], hb[s][:, kc * 128:(kc + 1) * 128], [("hb", s), "ident"], [("ps", b)])
                gb = gcol[:, kg * 8:(kg + 1) * 8].unsqueeze(2).to_broadcast([128, 8, 128])
                tt("dve", hTs[sl][:, kg * 8:(kg + 1) * 8, (t % 4) * 128:(t % 4 + 1) * 128], tp, gb, ALU.mult,
                   [("ps", b), "gcol"], [("hTs", sl)], )
            if t % 4 == 3:
                for kq in range(4):
                    dma("sp", hT_d[kq * 1024:(kq + 1) * 1024, g4 * 512:(g4 + 1) * 512].rearrange("(k p) t -> p k t", p=128),
                        hTs[sl][:, kq * 8:(kq + 1) * 8, :], [("hTs", sl)], [("hT_d", g4, kq)], ("hTs", sl))
        phase_end()

    def phase_proj_tok(inT_d, KC, T, W, NB, KH, post, setup=None, in_res="hT_d"):
        KCh = KC // KH
        inT = A.take(KC * T).rearrange("p (k t) -> p k t", k=KC)
        Wt = [A.take(KCh * 512).rearrange("p (k n) -> p k n", k=KCh) for _ in range(2)]
        ctx = setup() if setup is not None else None
        NTT = T // 128
        gi = 0
        KG = 8
        ngr = (KC + KG - 1) // KG
        seq = [(tg, cb, kh) for tg in range(S // T) for cb in range(NB) for kh in range(KH)]

        def issue_w(i):
            _, cb_, kh_ = seq[i]
            ws_ = i % 2
            dma("pool", Wt[ws_], W[kh_ * KCh * 128:(kh_ + 1) * KCh * 128, cb_ * 512:(cb_ + 1) * 512].rearrange("(k p) n -> p k n", p=128),
                [], [("W", ws_)], ("W", ws_))
        issue_w(0)
        cur_tg = -1
        for i, (tg, cb, kh) in enumerate(seq):
            if tg != cur_tg:
                cur_tg = tg
                for g in range(ngr):
                    k0, k1 = g * KG, min(KC, (g + 1) * KG)
                    dma("sp", inT[:, k0:k1, :],
                        inT_d[k0 * 128:k1 * 128, tg * T:(tg + 1) * T].rearrange("(k p) t -> p k t", p=128),
                        [(in_res, a, b) for a in range(4) for b in range(4)] if in_res == "hT_d" else [in_res],
                        [("inT", g)], ("inT", g))
            if i + 1 < len(seq):
                issue_w(i + 1)
            ws = i % 2
            for t in range(NTT):
                if KH == 1:
                    bank = (gi + t) % 8
                else:
                    bank = (cb % 2) * 4 + t
                for kc in range(KCh):
                    kk = kh * KCh + kc
                    mm(PS[bank][:, :], inT[:, kk, t * 128:(t + 1) * 128], Wt[ws][:, kc, :],
                       start=(kk == 0), stop=(kk == KC - 1),
                       reads=[("inT", kk // KG), ("W", ws)], writes=[("ps", bank)])
                if kh == KH - 1:
                    post(ctx, tg, cb, tg * NTT + t, bank)
            if KH == 1:
                gi += NTT
        phase_end()

    def make_resid_post(src):
        st = {"i": 0}

        def setup():
            xr = [A.take(1024, F32) for _ in range(4)]
            xo = [A.take(1024, F32) for _ in range(4)]
            return xr, xo

        def post(ctx, tg, cb, t, bank):
            xr, xo = ctx
            s = st["i"] % 4
            st["i"] += 1
            dma("sp", xr[s], src[t * 128:(t + 1) * 128, cb * 512:(cb + 1) * 512], [("x", t, cb)], [("xr", s)], ("xr", s))
            tt("dve", xo[s], PS[bank][:, :], xr[s], ALU.add, [("ps", bank), ("xr", s)], [("xo", s)])
            dma("sp", y[t * 128:(t + 1) * 128, cb * 512:(cb + 1) * 512], xo[s], [("xo", s)], [("x", t, cb)], ("xo", s))
        return setup, post

    def make_qkv_post(kinds, T):
        st = {"i": 0}
        NTT = T // 128
        gains = []
        for k in kinds:
            if k is not None and all(k is not g for g in gains):
                gains.append(k)

        def setup():
            cosT = A.take(NTT * 128 * 2, F32).rearrange("p (t d) -> p t d", t=NTT)
            sinT = A.take(NTT * 128 * 2, F32).rearrange("p (t d) -> p t d", t=NTT)
            gb = {}
            for gi_, g in enumerate(gains):
                gt = A.take(1024, F32)
                for r in range(4):
                    dma("sp", gt[:, r * 128:(r + 1) * 128], g.to_broadcast([128, 128]), [], [("gt", gi_)], ("c", 2 + gi_),
                        partial=True)
                gb[id(g)] = (gi_, gt)
            raw = [A.take(1024, F32) for _ in range(4)]
            sq = [A.take(1024, F32) for _ in range(4)]
            y1 = [A.take(1024, F32) for _ in range(4)]
            t1 = [A.take(1024, F32) for _ in range(4)]
            t2 = [A.take(1024, F32) for _ in range(4)]
            ob = [A.take(512) for _ in range(4)]
            sm = A.take(128, F32)
            return dict(cosT=cosT, sinT=sinT, gb=gb, raw=raw, sq=sq, y1=y1, t1=t1, t2=t2, ob=ob, sm=sm, tg=-1)

        def post(c, tg, cb, t, bank):
            if c["tg"] != tg:
                c["tg"] = tg
                dma("sp", c["cosT"], c_cos[tg * T:(tg + 1) * T, :].rearrange("(t p) d -> p t d", p=128), [], ["cosT"], ("c", 8))
                dma("sp", c["sinT"], c_sin[tg * T:(tg + 1) * T, :].rearrange("(t p) d -> p t d", p=128), [], ["sinT"], ("c", 9))
            i = st["i"]
            st["i"] += 1
            o = i % 4
            s = i % 4
            ti = t % NTT
            k = kinds[cb]
            ob = c["ob"][o]
            if k is None:
                cp("act", ob, PS[bank][:, :], [("ps", bank)], [("ob", o)])
            else:
                gi_, gt = c["gb"][id(k)]
                raw, sq, y1, t1, t2 = c["raw"][s], c["sq"][s], c["y1"][s], c["t1"][s], c["t2"][s]
                ss = c["sm"][:, 16 * s:16 * s + 4]
                sd = c["sm"][:, 16 * s + 4:16 * s + 8]
                rs = c["sm"][:, 16 * s + 8:16 * s + 12]
                cp("act", raw, PS[bank][:, :], [("ps", bank)], [("raw", s)])
                act(sq, PS[bank][:, :], AF.Square, [("ps", bank)], [("sq", s)])
                tred(ss, sq.rearrange("p (h d) -> p h d", h=4), [("sq", s)], [("ss", s)])
                act(sd, ss, AF.Sqrt, [("ss", s)], [("sd", s)], scale=1.0 / 128, bias=EPS)
                recip(rs, sd, [("sd", s)], [("rs", s)])
                tt("pool", y1, raw, gt, ALU.mult, [("raw", s), ("gt", gi_)], [("y1", s)])
                y1v = y1.rearrange("p (h d) -> p h d", h=4)
                t1v = t1.rearrange("p (h d) -> p h d", h=4)
                t2v = t2.rearrange("p (h d) -> p h d", h=4)
                cb_ = c["cosT"][:, ti:ti + 1, :].to_broadcast([128, 4, 128])
                tt("dve", t1v, y1v, cb_, ALU.mult, [("y1", s), "cosT"], [("t1", s)])
                sa = c["sinT"][:, ti:ti + 1, 0:64].to_broadcast([128, 4, 64])
                sb = c["sinT"][:, ti:ti + 1, 64:128].to_broadcast([128, 4, 64])
                tt("pool", t2v[:, :, 0:64], y1v[:, :, 64:128], sa, ALU.mult, [("y1", s), "sinT"], [("t2", s)])
                tt("pool", t2v[:, :, 64:128], y1v[:, :, 0:64], sb, ALU.mult, [("y1", s), "sinT"], [("t2", s)], )
                tt("dve", t1, t1, t2, ALU.add, [("t1", s), ("t2", s)], [("t1", s)])
                rb = rs.unsqueeze(2).to_broadcast([128, 4, 128])
                tt("dve", ob.rearrange("p (h d) -> p h d", h=4), t1v, rb, ALU.mult, [("t1", s), ("rs", s)], [("ob", o)])
            dma("sp", qkv_d[t * 128:(t + 1) * 128, cb * 512:(cb + 1) * 512], ob, [("ob", o)], [("qkv_d", cb, t)], ("ob", o))
        return setup, post

    def phase_ffn_in(l):
        T = 1024
        inT = A.take(32 * T).rearrange("p (k t) -> p k t", k=32)
        Wg = [A.take(32 * 256).rearrange("p (k n) -> p k n", k=32) for _ in range(2)]
        Wu = [A.take(32 * 256).rearrange("p (k n) -> p k n", k=32) for _ in range(2)]
        sg = [A.take(1024, F32) for _ in range(2)]
        ao = [A.take(1024) for _ in range(3)]
        wi = 0
        pi = 0
        ai = 0
        for tg in range(S // T):
            for g in range(4):
                dma("sp", inT[:, g * 8:(g + 1) * 8, :],
                    hT_d[g * 1024:(g + 1) * 1024, tg * T:(tg + 1) * T].rearrange("(k p) t -> p k t", p=128),
                    [("hT_d", a, b) for a in range(4) for b in range(4)], [("inT", g)], ("inT", g))
            for hb_ in range(HID // 256):
                ws = wi % 2
                wi += 1
                dma("pool", Wg[ws], w_gate[l, :, hb_ * 256:(hb_ + 1) * 256].rearrange("(k p) n -> p k n", p=128), [], [("Wg", ws)], ("Wg", ws))
                dma("pool", Wu[ws], w_up[l, :, hb_ * 256:(hb_ + 1) * 256].rearrange("(k p) n -> p k n", p=128), [], [("Wu", ws)], ("Wu", ws))
                for hc in range(2):
                    a = ai % 3
                    ai += 1
                    for th in range(2):
                        bg = (pi % 4) * 2
                        bu = bg + 1
                        pi += 1
                        for kc in range(32):
                            mm(PS[bg][:, :], Wg[ws][:, kc, hc * 128:(hc + 1) * 128], inT[:, kc, th * 512:(th + 1) * 512],
                               kc == 0, kc == 31, [("inT", kc // 8), ("Wg", ws)], [("ps", bg)])
                        for kc in range(32):
                            mm(PS[bu][:, :], Wu[ws][:, kc, hc * 128:(hc + 1) * 128], inT[:, kc, th * 512:(th + 1) * 512],
                               kc == 0, kc == 31, [("inT", kc // 8), ("Wu", ws)], [("ps", bu)])
                        s = pi % 2
                        act(sg[s], PS[bg][:, :], AF.Silu, [("ps", bg)], [("sg", s)])
                        tt("dve", ao[a][:, th * 512:(th + 1) * 512], PS[bu][:, :], sg[s], ALU.mult, [("ps", bu), ("sg", s)], [("ao", a)],)
                    r0 = hb_ * 256 + hc * 128
                    dma("sp", actT_d[r0:r0 + 128, tg * T:(tg + 1) * T], ao[a], [("ao", a)], [("actT_d", hb_, hc, tg)], ("ao", a))
        phase_end()

    def load_tok(dst, col0, width, res, key):
        return dma("sp", dst, qkv_d[:, col0:col0 + width].rearrange("(t p) c -> p t c", p=128), ["qkv_all"], [res], key)

    def transpose_tok(src, dstT, res_src, res_dst, evac_eng):
        for r in range(2):
            tp = psb(6)
            for k in range(8):
                tr(tp[:, k, :], src[:, r * 8 + k, :], [res_src, "ident"], [("ps", 6)])
            cp(evac_eng, dstT[:, r * 1024:(r + 1) * 1024], PS[6][:, :].bitcast(BF16), [("ps", 6)], [res_dst])

    def phase_attn_even(l):
        e = l // 2
        lam_init = 0.8 - 0.6 * math.exp(-0.3 * l)
        qtok = [A.take(2048).rearrange("p (t d) -> p t d", t=16) for _ in range(2)]
        ktok = [A.take(2048).rearrange("p (t d) -> p t d", t=16) for _ in range(2)]
        vA = [A.take(16 * 258).rearrange("p (t d) -> p t d", t=16) for _ in range(2)]
        vB = [A.take(16 * 130).rearrange("p (t d) -> p t d", t=16) for _ in range(2)]
        qT = [A.take(2048) for _ in range(2)]
        kT = [A.take(2048) for _ in range(2)]
        PT = [A.take(512) for _ in range(4)]
        mlong = A.take(3968)
        o1 = A.take(16 * 256 * 2, F32).rearrange("p (t d) -> p t d", t=16)
        res = [A.take(512, F32) for _ in range(4)]
        resb = [A.take(256) for _ in range(8)]
        osb = [A.take(520, F32) for _ in range(8)]
        sm2 = A.take(256, F32)
        STB = [4, 5, 7]
        pending = []
        osi = 0
        junk = A.take(512, F32)
        catS = [A.take(2 * 2048).rearrange("p (c t) -> p c t", c=2) for _ in range(2)]
        lbc = A.take(1024, F32).rearrange("p (a d) -> p a d", a=4)
        sublng = A.take(512, F32)
        sm = A.take(64, F32)
        dma("sp", mlong, c_mlong[:, :], [], ["mlong"], ("c", 1))
        dma("sp", lbc, diff_lambda[e:e + 1, :].to_broadcast([128, 512]).rearrange("p (a d) -> p a d", a=4), [], ["lbc"], ("c", 2))
        dma("sp", sublng, diff_subln[e:e + 1, :].to_broadcast([128, 256]), [], ["sublng"], ("c", 3))
        for s in range(2):
            P.emit("pool", (lambda s_: (lambda en: en.memset(vA[s_][:, :, 256:257], 1.0)))(s), [], [("vA1", s)])
            P.emit("pool", (lambda s_: (lambda en: en.memset(vB[s_][:, :, 128:129], 1.0)))(s), [], [("vB1", s)])
        s1, s2, e1, e2, lam, neglam = [sm[:, i:i + 1] for i in range(6)]
        tt("dve", junk[:, 0:128], lbc[:, 0, :], lbc[:, 1, :], ALU.mult, ["lbc"], ["junk"])
        tred(s1, junk[:, 0:128], ["junk"], ["s1"])
        tt("dve", junk[:, 128:256], lbc[:, 2, :], lbc[:, 3, :], ALU.mult, ["lbc"], ["junk2"])
        tred(s2, junk[:, 128:256], ["junk2"], ["s2"])
        act(e1, s1, AF.Exp, ["s1"], ["e1"])
        act(e2, s2, AF.Exp, ["s2"], ["e2"])
        tt("dve", lam, e1, e2, ALU.subtract, ["e1", "e2"], ["lam"])
        ts("dve", neglam, lam, lam_init, ALU.add, ["lam"], ["neglam"], s2=-1.0, op1=ALU.mult)
        ts("dve", sublng, sublng, 1.0 - lam_init, ALU.mult, ["sublng"], ["sublng"])

        maps = []
        for h in range(8):
            for t in range(2):
                maps.append(("diff", h, t, h * 256 + t * 128, 2048 + h * 256 + t * 128, 4096 + h * 256, 256))
        for h in range(16):
            maps.append(("dil", h, 0, 6144 + h * 128, 8192 + h * 128, 10240 + h * 128, 128))
        pti = 0
        rsi = 0
        vai = 0
        vbi = 0
        csi = 0
        info = []
        for mi, (kind, h, t, qc, kc_, vc, vw) in enumerate(maps):
            if kind == "diff":
                if t == 0:
                    vai += 1
                info.append((vA[vai % 2], [("vA", vai % 2), ("vA1", vai % 2)], t == 0, ("vA", vai % 2)))
            else:
                vbi += 1
                info.append((vB[vbi % 2], [("vB", vbi % 2), ("vB1", vbi % 2)], True, ("vB", vbi % 2)))

        def prep_load(mi):
            kind, h, t, qc, kc_, vc, vw = maps[mi]
            s_ = mi % 2
            vext_, _, doload, vkey = info[mi]
            load_tok(qtok[s_], qc, 128, ("qtok", s_), ("qtok", s_))
            load_tok(ktok[s_], kc_, 128, ("ktok", s_), ("ktok", s_))
            if doload:
                dma("sp", vext_[:, :, 0:vw], qkv_d[:, vc:vc + vw].rearrange("(t p) c -> p t c", p=128), ["qkv_all"], [vkey], vkey)

        def prep_tr(mi):
            s_ = mi % 2
            transpose_tok(qtok[s_], qT[s_], ("qtok", s_), ("qT", s_), "dve")
            transpose_tok(ktok[s_], kT[s_], ("ktok", s_), ("kT", s_), "act")
        prep_load(0)
        prep_tr(0)
        for mi, (kind, h, t, qc, kc_, vc, vw) in enumerate(maps):
            s = mi % 2
            vext, vres, _, _ = info[mi]
            if mi + 1 < len(maps):
                prep_load(mi + 1)
            if kind == "diff" and t == 0 or kind == "dil":
                csi += 1
            cs = csi % 2
            for qb in range(4):
                kts = list(range(16))
                if kind == "dil":
                    kts = [kt for kt in range(16) if (kt - (qb * 4 + 3)) <= 8 and ((qb * 4) - kt) <= 8]

                def st_mm(jj):
                    b = STB[jj % 3]
                    kt_ = kts[jj]
                    mm(PS[b][:, :], kT[s][:, kt_ * 128:(kt_ + 1) * 128], qT[s][:, qb * 512:(qb + 1) * 512], True, True,
                       [("kT", s), ("qT", s)], [("ps", b)])
                for jj in range(min(3, len(kts))):
                    st_mm(jj)
                for j, kt in enumerate(kts):
                    b = STB[j % 3]
                    p = pti % 4
                    pti += 1
                    act(PT[p], PS[b][:, :], AF.Exp, [("ps", b)], [("PT", p)], scale=SCALE)
                    if kind == "dil":
                        off = (qb * 4 - kt + 15) * 128
                        tt("dve", PT[p], PT[p], mlong[:, off:off + 512], ALU.mult, [("PT", p), "mlong"], [("PT", p)])
                    for qi in range(4):
                        mm(PS[qi][:, 0:vw + 1], PT[p][:, qi * 128:(qi + 1) * 128], vext[:, kt, 0:vw + 1],
                           j == 0, j == len(kts) - 1, [("PT", p)] + vres, [("ps", qi)])
                    if j + 3 < len(kts):
                        st_mm(j + 3)
                    if j == 3 and pending:
                        for f_ in pending:
                            f_()
                        del pending[:]
                for qi in range(4):
                    qt = qb * 4 + qi
                    k = osi % 8
                    osi += 1
                    ob_ = osb[k]
                    cp("dve" if qi % 2 == 0 else "act", ob_[:, 0:vw + 1], PS[qi][:, 0:vw + 1], [("ps", qi)], [("osb", k)])
                    r = rsi % 8
                    rsi += 1
                    rz, rz2, ss, sd, rs = [sm2[:, 8 * r + q_:8 * r + q_ + 1] for q_ in range(5)]
                    recip(rz, ob_[:, vw:vw + 1], [("osb", k)], [("rz", r)])
                    if kind == "diff" and t == 0:
                        ts("dve", o1[:, qt, :], ob_[:, 0:256], rz, ALU.mult, [("osb", k), ("rz", r)], [("o1", qt)])
                        continue
                    if kind == "diff":
                        r4 = r % 4
                        tt("dve", rz2, rz, neglam, ALU.mult, [("rz", r), "neglam"], [("rz2", r)])
                        stt(res[r4], ob_[:, 0:256], rz2, o1[:, qt, :], ALU.mult, ALU.add, [("osb", k), ("rz2", r), ("o1", qt)], [("res", r4)])
                        act(junk[:, 0:256], res[r4], AF.Square, [("res", r4)], ["junk", ("ss", r)], accum=ss)
                        act(sd, ss, AF.Sqrt, [("ss", r)], [("sd", r)], scale=1.0 / 256, bias=EPS)
                        recip(rs, sd, [("sd", r)], [("rs", r)])
                        stt(resb[r], res[r4], rs, sublng, ALU.mult, ALU.mult, [("res", r4), ("rs", r), "sublng"], [("resb", r)])
                        nch = 2
                    else:
                        ts("dve", resb[r][:, 0:128], ob_[:, 0:128], rz, ALU.mult, [("osb", k), ("rz", r)], [("resb", r)])
                        nch = 1

                    def make_fin(r_, nch_, cs_, qt_):
                        def f_():
                            tp = psb(6)
                            for c in range(nch_):
                                tr(tp[:, c, :], resb[r_][:, c * 128:(c + 1) * 128], [("resb", r_), "ident"], [("ps", 6)])
                            cp("act", catS[cs_][:, 0:nch_, qt_ * 128:(qt_ + 1) * 128], tp[:, 0:nch_, :], [("ps", 6)], [("catS", cs_)])
                        return f_
                    pending.append(make_fin(r, nch, cs, qt))
                if qb == 2 and mi + 1 < len(maps):
                    prep_tr(mi + 1)

            def make_store(kind_, h_, cs_):
                def f_():
                    if kind_ == "diff":
                        dma("sp", catT_d[h_ * 256:(h_ + 1) * 256, :].rearrange("(c p) t -> p c t", p=128), catS[cs_],
                            [("catS", cs_)], [("catT_d", h_)], ("catS", cs_))
                    else:
                        r0 = 2048 + h_ * 128
                        dma("sp", catT_d[r0:r0 + 128, :], catS[cs_][:, 0, :], [("catS", cs_)], [("catT_d", 8 + h_)], ("catS", cs_))
                return f_
            if (kind == "diff" and t == 1) or kind == "dil":
                pending.append(make_store(kind, h, cs))
        for f_ in pending:
            f_()
        del pending[:]
        phase_end()

    def phase_attn_odd(l):
        o = l // 2
        qtok = [A.take(16 * 512).rearrange("p (t d) -> p t d", t=16) for _ in range(2)]
        ktok = [A.take(2048).rearrange("p (t d) -> p t d", t=16) for _ in range(2)]
        vB = [A.take(16 * 130).rearrange("p (t d) -> p t d", t=16) for _ in range(2)]
        qT4 = [A.take(4 * 2048).rearrange("p (h t) -> p h t", h=4) for _ in range(2)]
        kT = [A.take(2048) for _ in range(2)]
        PT3 = [A.take(3 * 512).rearrange("p (j c) -> p j c", j=3) for _ in range(4)]
        tri = A.take(256)
        resb = [A.take(512).rearrange("p (h d) -> p h d", h=4) for _ in range(4)]
        osb = [A.take(2 * 520, F32).rearrange("p (b c) -> p b c", b=2) for _ in range(4)]
        STB = [4, 5, 7]
        catS = [A.take(4 * 2048).rearrange("p (h t) -> p h t", h=4) for _ in range(2)]
        sbc = A.take(64, F32)
        esink = A.take(64, F32)
        sm = A.take(64, F32)
        dma("sp", tri, c_tri[:, :], [], ["tri"], ("c", 1))
        dma("sp", sbc, win_sink[o:o + 1, :].to_broadcast([128, 32]), [], ["sbc"], ("c", 2))
        act(esink, sbc, AF.Exp, ["sbc"], ["esink"])
        for s in range(2):
            P.emit("pool", (lambda s_: (lambda en: en.memset(vB[s_][:, :, 128:129], 1.0)))(s), [], [("vB1", s)])
        rsi = 0
        for g in range(8):
            s = g % 2
            load_tok(qtok[s], g * 512, 512, ("qtok", s), ("qtok", s))
            load_tok(ktok[s], 4096 + g * 128, 128, ("ktok", s), ("ktok", s))
            dma("sp", vB[s][:, :, 0:128], qkv_d[:, 5120 + g * 128:5120 + (g + 1) * 128].rearrange("(t p) c -> p t c", p=128),
                ["qkv_all"], [("vB", s)], ("vB", s))
            for t in range(16):
                tp = psb(6)
                for hh in range(4):
                    tr(tp[:, hh, :], qtok[s][:, t, hh * 128:(hh + 1) * 128], [("qtok", s), "ident"], [("ps", 6)])
                cp("dve" if t % 2 == 0 else "act", qT4[s][:, :, t * 128:(t + 1) * 128], tp[:, 0:4, :], [("ps", 6)], [("qT4", s)])
            transpose_tok(ktok[s], kT[s], ("ktok", s), ("kT", s), "act")
            def stageA(qt):
                kts = [kt for kt in (qt - 1, qt, qt + 1) if 0 <= kt <= 15]
                pp = qt % 4
                for j, kt in enumerate(kts):
                    b = STB[j]
                    mm(PS[b][:, :], kT[s][:, kt * 128:(kt + 1) * 128], qT4[s][:, :, qt * 128:(qt + 1) * 128], True, True,
                       [("kT", s), ("qT4", s)], [("ps", b)])
                    act(PT3[pp][:, j, :], PS[b][:, :], AF.Exp, [("ps", b)], [("PT3", pp, j)], scale=SCALE)
                    if kt != qt:
                        m = tri[:, 0:128] if kt < qt else tri[:, 128:256]
                        pv = PT3[pp][:, j, :].rearrange("p (h q) -> p h q", h=4)
                        tt("dve", pv, pv, m.unsqueeze(1).to_broadcast([128, 4, 128]), ALU.mult, [("PT3", pp, j), "tri"], [("PT3", pp, j)])

            def stageB(qt):
                kts = [kt for kt in (qt - 1, qt, qt + 1) if 0 <= kt <= 15]
                pp = qt % 4
                r = qt % 4
                for hh in range(4):
                    bank = (qt % 2) * 2 + hh // 2
                    c0 = (hh % 2) * 130
                    for j, kt in enumerate(kts):
                        mm(PS[bank][:, c0:c0 + 129], PT3[pp][:, j, hh * 128:(hh + 1) * 128], vB[s][:, kt, 0:129],
                           j == 0, j == len(kts) - 1, [("PT3", pp, j), ("vB", s), ("vB1", s)], [("ps", bank)])
                for bi in range(2):
                    bank = (qt % 2) * 2 + bi
                    cp("dve" if bi == 0 else "act", osb[r][:, bi, 0:259], PS[bank][:, 0:259], [("ps", bank)], [("osb", r, bi)])
                for hh in range(4):
                    bi = hh // 2
                    c0 = (hh % 2) * 130
                    den = sm[:, 8 * r + hh:8 * r + hh + 1]
                    rz = sm[:, 8 * r + 4 + hh:8 * r + 5 + hh]
                    hd = g * 4 + hh
                    tt("dve", den, osb[r][:, bi, c0 + 128:c0 + 129], esink[:, hd:hd + 1], ALU.add, [("osb", r, bi), "esink"], [("den", r, hh)])
                    recip(rz, den, [("den", r, hh)], [("rz", r, hh)])
                    ts("dve", resb[r][:, hh, :], osb[r][:, bi, c0:c0 + 128], rz, ALU.mult, [("osb", r, bi), ("rz", r, hh)], [("resb", r)])

            def stageC(qt):
                r = qt % 4
                tp = psb(6)
                for hh in range(4):
                    tr(tp[:, hh, :], resb[r][:, hh, :], [("resb", r), "ident"], [("ps", 6)])
                cp("act", catS[s][:, :, qt * 128:(qt + 1) * 128], tp[:, 0:4, :], [("ps", 6)], [("catS", s)])

            stageA(0)
            for qt in range(16):
                if qt + 1 < 16:
                    stageA(qt + 1)
                stageB(qt)
                if qt >= 1:
                    stageC(qt - 1)
            stageC(15)
            dma("sp", catT_d[g * 512:(g + 1) * 512, :].rearrange("(h p) t -> p h t", p=128), catS[s], [("catS", s)], [("catT_d", g)], ("catS", s))
        phase_end()

    phases = []
    for l in range(n_layers):
        phases += [(l, "norm1"), (l, "mixin"), (l, "attn"), (l, "mixout"), (l, "norm2"), (l, "ffnin"), (l, "ffnout")]
    xsrc = x_in
    if start_at is not None:
        phases = phases[phases.index(start_at):]
    for (l, ph) in phases:
        if ph == "norm1":
            phase_norm(xsrc, mix_norm[l, :])
        elif ph == "mixin":
            if l % 2 == 0:
                e = l // 2
                gq, gk = diff_q_norm[e:e + 1, :], diff_k_norm[e:e + 1, :]
                hq, hk = dil_q_norm[e:e + 1, :], dil_k_norm[e:e + 1, :]
                kinds = [gq] * 4 + [gk] * 4 + [None] * 4 + [hq] * 4 + [hk] * 4 + [None] * 4
                setup, post = make_qkv_post(kinds, 1024)
                phase_proj_tok(hT_d, 32, 1024, hy_w_in[e, :, :], 24, 1, post, setup)
            else:
                o = l // 2
                gq, gk = win_q_norm[o:o + 1, :], win_k_norm[o:o + 1, :]
                kinds = [gq] * 8 + [gk] * 2 + [None] * 2
                setup, post = make_qkv_post(kinds, 1024)
                phase_proj_tok(hT_d, 32, 1024, win_w_in[o, :, :], 12, 1, post, setup)
        elif ph == "attn":
            if l % 2 == 0:
                phase_attn_even(l)
            else:
                phase_attn_odd(l)
        elif ph == "mixout":
            setup, post = make_resid_post(xsrc)
            W = hy_w_out[l // 2, :, :] if l % 2 == 0 else win_w_out[l // 2, :, :]
            phase_proj_tok(catT_d, 32, 1024, W, 8, 1, post, setup, in_res="catT_all")
            xsrc = y
        elif ph == "norm2":
            phase_norm(xsrc, ffn_norm[l, :])
        elif ph == "ffnin":
            phase_ffn_in(l)
        elif ph == "ffnout":
            HK = HID // 2
            setup, post = make_resid_post(xsrc)
            phase_proj_tok(actT_d[0:HK, :], 43, 1024, w_down[l, 0:HK, :], 8, 1, post, setup, in_res="actT_all")
            setup, post = make_resid_post(y)
            phase_proj_tok(actT_d[HK:HID, :], 43, 1024, w_down[l, HK:HID, :], 8, 1, post, setup, in_res="actT_all")
        if stop_after == (l, ph):
            break

    P.finalize()
    with nc.Block() as block:
        @block.tensor
        def _(e):
            P.replay("pe", e)

        @block.scalar
        def _(e):
            P.replay("act", e)

        @block.vector
        def _(e):
            P.replay("dve", e)

        @block.gpsimd
        def _(e):
            P.replay("pool", e)

        @block.sync
        def _(e):
            P.replay("sp", e)
    return nc


def host_consts():
    bf = ml_dtypes.bfloat16
    ident = np.eye(128, dtype=np.float32).astype(bf)
    inv_freq = (10000.0 ** (-np.arange(0, 128, 2, dtype=np.float32) / 128)).astype(np.float32)
    ang = np.arange(S, dtype=np.float32)[:, None] * inv_freq[None, :]
    ang = np.concatenate([ang, ang], axis=-1)
    cos = np.cos(ang).astype(np.float32)
    sin = np.sin(ang).astype(np.float32)
    sin[:, :64] *= -1.0
    kp = np.arange(128)[:, None]
    u = np.arange(3968)[None, :]
    d = kp - u + 1920
    ad = np.abs(d)
    cnt = (ad <= 64).astype(np.float32) + ((d % 4 == 0) & (ad <= 256)) + ((d % 16 == 0) & (ad <= 1024))
    mlong = cnt.astype(bf)
    qp = np.arange(128)[None, :]
    tri = np.concatenate([(kp >= qp), (kp <= qp)], axis=1).astype(np.float32).astype(bf)
    return dict(c_ident=ident, c_cos=cos, c_sin=sin, c_mlong=mlong, c_tri=tri)


_CACHE = {}


def kernel(**inputs):
    n = 8
    if "nc" not in _CACHE:
        _CACHE["nc"] = build()
    nc = _CACHE["nc"]
    consts = host_consts()
    x = np.asarray(inputs["x"], dtype=np.float32)
    shared = {k: np.asarray(v) for k, v in inputs.items() if k != "x"}
    shared["diff_lambda"] = shared["diff_lambda"].reshape(2, 512)
    in_maps = []
    for c in range(n):
        m = {"x": x[c]}
        m.update(shared)
        m.update(consts)
        in_maps.append(m)
    res = run_bass_kernel_spmd(nc, in_maps, core_ids=list(range(n)))
    return np.stack([r["y"] for r in res.results], axis=0).astype(np.float32)
```

```python
import math
import numpy as np
import ml_dtypes
import concourse.bass as bass
import concourse.mybir as mybir
from concourse.bass_utils import run_bass_kernel_spmd

F32 = mybir.dt.float32
BF16 = mybir.dt.bfloat16
AF = mybir.ActivationFunctionType
ALU = mybir.AluOpType
AX = mybir.AxisListType

D = 4096
S = 2048
NT = S // 128
HID = 11008
EVEN_IN = 12288
ODD_IN = 6144
EPS = 1e-6
SCALE = 128 ** -0.5
ENGS = ("pe", "act", "dve", "pool", "sp")
SAME_ENGINE_SYNC = True


class Op:
    __slots__ = ("eng", "fn", "deps", "dsem", "need_inc", "sem", "val")


class Prog:
    def __init__(self, nc):
        self.nc = nc
        self.ops = {e: [] for e in ENGS}
        self.res = {}
        self.dma_latest = {}
        self.esem = {}
        self.dsem_h = {}

    def emit(self, eng, fn, reads=(), writes=(), dsem=None, partial=False, extra_deps=()):
        op = Op()
        op.eng = eng
        op.fn = fn
        op.dsem = dsem
        op.need_inc = dsem is not None
        op.sem = None
        op.val = 0
        deps = set(extra_deps)
        key = dsem if dsem is not None else eng
        for r in reads:
            st = self.res.get(r)
            if st is None:
                st = self.res[r] = [{}, {}, {}]
            deps.update(st[1].values())
            st[2][key] = op
        for w in writes:
            st = self.res.get(w)
            if st is None:
                st = self.res[w] = [{}, {}, {}]
            if st[2]:
                st[0] = st[2]
                st[2] = {}
                st[1] = {}
            deps.update(st[0].values())
            if not partial:
                deps.update(st[1].values())
            st[1][key] = op
        fdeps = []
        for d in deps:
            if d is op:
                continue
            if d.dsem is None and d.eng == eng:
                if eng == "pe" or not SAME_ENGINE_SYNC:
                    continue
            d.need_inc = True
            fdeps.append(d)
        op.deps = fdeps
        self.ops[eng].append(op)
        if dsem is not None:
            self.dma_latest[dsem] = op
        return op

    def barrier(self, tiny):
        bs = {}
        for e in ENGS:
            w = [("ps", 7)] if e == "pe" else ()
            bs[e] = self.emit(e, tiny[e][0], writes=w, extra_deps=list(self.dma_latest.values()),
                              dsem=("bar_sp",) if e == "sp" else None)
            bs[e].need_inc = True
        for e in ENGS:
            self.emit(e, tiny[e][1], extra_deps=[bs[x] for x in ENGS if x != e],
                      dsem=("bar_sp2",) if e == "sp" else None)
        self.res = {}
        self.dma_latest = {("bar_sp2",): self.dma_latest[("bar_sp2",)]}

    def finalize(self):
        nc = self.nc
        LIM = 30000
        cnt = {}
        cur = {}
        for e in ENGS:
            for op in self.ops[e]:
                if not op.need_inc:
                    continue
                key = op.dsem if op.dsem is not None else ("eng", e)
                step = 16 if op.dsem is not None else 1
                if key not in cur or cnt[key] + step > LIM:
                    cur[key] = nc.alloc_semaphore("s%d" % len(self.esem))
                    self.esem[len(self.esem)] = cur[key]
                    cnt[key] = 0
                cnt[key] += step
                op.sem = cur[key]
                op.val = cnt[key]

    def replay(self, eng, engine):
        seen = {}
        for op in self.ops[eng]:
            need = {}
            for d in op.deps:
                k = id(d.sem)
                if seen.get(k, 0) >= d.val:
                    continue
                if k not in need or need[k][1] < d.val:
                    need[k] = (d.sem, d.val)
            for k, (sem, val) in need.items():
                engine.wait_ge(sem, val)
                seen[k] = val
            ins = op.fn(engine)
            if op.need_inc:
                ins.then_inc(op.sem, 16 if op.dsem is not None else 1)


class Arena:
    def __init__(self, handle, n):
        self.h = handle
        self.n = n
        self.base = 0
        self.off = 0

    def take(self, nelem_bf16, dtype=BF16):
        o = (self.off + 15) // 16 * 16
        assert o + nelem_bf16 <= self.n, ("arena overflow", o, nelem_bf16, self.n)
        self.off = o + nelem_bf16
        v = self.h[:, o:o + nelem_bf16]
        if dtype == F32:
            v = v.bitcast(F32)
        return v

    def mark_persistent(self):
        self.base = self.off

    def reset(self):
        self.off = self.base


def build(n_layers=4, stop_after=None, debug=False, decl=None, start_at=None):
    nc = bass.Bass("TRN2", target_bir_lowering=False)
    dt = nc.dram_tensor

    decl = decl or {}

    def inp(name, shape, dtype=F32):
        shape = list(shape)
        if name in decl:
            if decl[name] == 0:
                shape = [1, 128, 128]
            else:
                shape[0] = decl[name]
        return dt(name, shape, dtype, kind="ExternalInput").ap()

    x_in = inp("x", [S, D])
    mix_norm = inp("mix_norm", [4, D])
    ffn_norm = inp("ffn_norm", [4, D])
    w_gate = inp("w_gate", [4, D, HID])
    w_up = inp("w_up", [4, D, HID])
    w_down = inp("w_down", [4, HID, D])
    hy_w_in = inp("hy_w_in", [2, D, EVEN_IN])
    hy_w_out = inp("hy_w_out", [2, D, D])
    diff_q_norm = inp("diff_q_norm", [2, 128])
    diff_k_norm = inp("diff_k_norm", [2, 128])
    diff_lambda = inp("diff_lambda", [2, 512])
    diff_subln = inp("diff_subln", [2, 256])
    dil_q_norm = inp("dil_q_norm", [2, 128])
    dil_k_norm = inp("dil_k_norm", [2, 128])
    win_w_in = inp("win_w_in", [2, D, ODD_IN])
    win_w_out = inp("win_w_out", [2, D, D])
    win_q_norm = inp("win_q_norm", [2, 128])
    win_k_norm = inp("win_k_norm", [2, 128])
    win_sink = inp("win_sink", [2, 32])
    c_ident = inp("c_ident", [128, 128], BF16)
    c_cos = inp("c_cos", [S, 128])
    c_sin = inp("c_sin", [S, 128])
    c_mlong = inp("c_mlong", [128, 3968], BF16)
    c_tri = inp("c_tri", [128, 256], BF16)
    y = dt("y", [S, D], F32, kind="ExternalOutput").ap()
    sk = dict(kind="ExternalOutput") if debug else {}
    hT_d = dt("hT_d", [D, S], BF16, **sk).ap()
    qkv_d = dt("qkv_d", [S, EVEN_IN], BF16, **sk).ap()
    catT_d = dt("catT_d", [D, S], BF16, **sk).ap()
    actT_d = dt("actT_d", [HID, S], BF16, **sk).ap()
    bar_d = dt("bar_d", [2, 128], F32).ap()

    NAR = 105600
    arena_h = nc.alloc_sbuf_tensor("arena", [128, NAR], BF16)
    A = Arena(arena_h, NAR)
    PS = [nc.alloc_psum_tensor("ps%d" % i, [128, 512], F32) for i in range(8)]

    def psb(i):
        return PS[i][:, :].bitcast(BF16).rearrange("p (a b) -> p a b", a=8)

    P = Prog(nc)
    ident = A.take(128)
    bsc = A.take(64, F32)
    A.mark_persistent()

    def dma(eng, out, in_, reads, writes, dsem, partial=False, slow=False):
        if slow:
            return P.emit(eng, lambda e: e.dma_start(out=out, in_=in_, allow_slow_non_contiguous=True), reads, writes,
                          dsem=dsem, partial=partial)
        return P.emit(eng, lambda e: e.dma_start(out=out, in_=in_), reads, writes, dsem=dsem, partial=partial)

    def mm(out, lhsT, rhs, start, stop, reads, writes):
        return P.emit("pe", lambda e: e.matmul(out, lhsT, rhs, start=start, stop=stop), reads, writes)

    def tr(out, in_, reads, writes):
        return P.emit("pe", lambda e: e.transpose(out, in_, ident), reads, writes)

    def act(out, in_, func, reads, writes, scale=None, bias=None, accum=None):
        def f(e):
            kw = {}
            if scale is not None:
                kw["scale"] = scale
            if bias is not None:
                kw["bias"] = bias
            if accum is not None:
                kw["accum_out"] = accum
            return e.activation(out, in_, func, **kw)
        return P.emit("act", f, reads, writes)

    def tt(eng, out, in0, in1, op, reads, writes):
        return P.emit(eng, lambda e: e.tensor_tensor(out, in0, in1, op), reads, writes)

    def ts(eng, out, in0, s1, op0, reads, writes, s2=None, op1=None):
        if op1 is None:
            return P.emit(eng, lambda e: e.tensor_scalar(out, in0, s1, None, op0), reads, writes)
        return P.emit(eng, lambda e: e.tensor_scalar(out, in0, s1, s2, op0, op1), reads, writes)

    def stt(out, in0, scalar, in1, op0, op1, reads, writes):
        return P.emit("dve", lambda e: e.scalar_tensor_tensor(out, in0, scalar, in1, op0, op1), reads, writes)

    def recip(out, in_, reads, writes):
        return P.emit("dve", lambda e: e.reciprocal(out, in_), reads, writes)

    def tred(out, in_, reads, writes):
        return P.emit("dve", lambda e: e.tensor_reduce(out, in_, AX.X, ALU.add), reads, writes)

    def cp(eng, out, in_, reads, writes):
        if eng == "act":
            return act(out, in_, AF.Copy, reads, writes)
        return P.emit(eng, lambda e: e.tensor_copy(out, in_), reads, writes)

    tiny = {
        "pe": (lambda e: e.matmul(PS[7][:, 0:128], ident, ident, start=True, stop=True),
               lambda e: e.matmul(PS[7][:, 128:256], ident, ident, start=True, stop=True)),
        "act": (lambda e: e.activation(bsc[:, 0:2], bsc[:, 2:4], AF.Copy),
                lambda e: e.activation(bsc[:, 4:6], bsc[:, 6:8], AF.Copy)),
        "dve": (lambda e: e.memset(bsc[:, 8:10], 0.0), lambda e: e.memset(bsc[:, 10:12], 0.0)),
        "pool": (lambda e: e.memset(bsc[:, 12:14], 0.0), lambda e: e.memset(bsc[:, 14:16], 0.0)),
        "sp": (lambda e: e.dma_start(out=bar_d[0:1, :], in_=c_cos[0:1, :]),
               lambda e: e.dma_start(out=bar_d[1:2, :], in_=c_cos[1:2, :])),
    }

    def phase_end():
        P.barrier(tiny)
        A.reset()

    dma("sp", ident, c_ident[:, :], [], ["ident"], ("c", 0))
    P.emit("dve", lambda e: e.memset(bsc[:, :], 0.0), [], ["bsc"])
    phase_end()

    def phase_norm(src, gain_row):
        xt = [A.take(8192, F32) for _ in range(4)]
        hb = [A.take(4096) for _ in range(4)]
        hTs = [A.take(32 * 512).rearrange("p (k t) -> p k t", k=32) for _ in range(2)]
        gcol = A.take(64, F32)
        ssb = A.take(64, F32)
        gsrc = gain_row.rearrange("(k p) -> p k", p=128)
        for kq in range(4):
            dma("sp", gcol[:, kq * 8:(kq + 1) * 8], gsrc[:, kq * 8:(kq + 1) * 8], [], ["gcol"], ("c", 1), partial=True, slow=True)
        for t in range(NT):
            s = t % 4
            g4 = t // 4
            sl = g4 % 2
            dma("sp", xt[s], src[t * 128:(t + 1) * 128, :], [("x", t, c) for c in range(8)], [("xt", s)], ("xt", s))
            ss = ssb[:, 4 * s:4 * s + 1]
            sd = ssb[:, 4 * s + 1:4 * s + 2]
            rs = ssb[:, 4 * s + 2:4 * s + 3]
            act(hb[s], xt[s], AF.Square, [("xt", s)], [("hb", s), ("ss", s)], accum=ss)
            act(sd, ss, AF.Sqrt, [("ss", s)], [("sd", s)], scale=1.0 / D, bias=EPS)
            recip(rs, sd, [("sd", s)], [("rs", s)])
            if t % 2 == 0:
                act(hb[s], xt[s], AF.Copy, [("xt", s), ("rs", s)], [("hb", s)], scale=rs)
            else:
                ts("dve", hb[s], xt[s], rs, ALU.mult, [("xt", s), ("rs", s)], [("hb", s)])
            for kg in range(4):
                b = kg % 2
                tp = psb(b)
                for k in range(8):
                    kc = kg * 8 + k
                    tr(tp[:, k, :], hb[s][:, kc * 128:(kc + 1) * 128], [("hb", s), "ident"], [("ps", b)])
                gb = gcol[:, kg * 8:(kg + 1) * 8].unsqueeze(2).to_broadcast([128, 8, 128])
                tt("dve", hTs[sl][:, kg * 8:(kg + 1) * 8, (t % 4) * 128:(t % 4 + 1) * 128], tp, gb, ALU.mult,
                   [("ps", b), "gcol"], [("hTs", sl)], )
            if t % 4 == 3:
                for kq in range(4):
                    dma("sp", hT_d[kq * 1024:(kq + 1) * 1024, g4 * 512:(g4 + 1) * 512].rearrange("(k p) t -> p k t", p=128),
                        hTs[sl][:, kq * 8:(kq + 1) * 8, :], [("hTs", sl)], [("hT_d", g4, kq)], ("hTs", sl))
        phase_end()

    def phase_proj_tok(inT_d, KC, T, W, NB, KH, post, setup=None, in_res="hT_d"):
        KCh = KC // KH
        inT = A.take(KC * T).rearrange("p (k t) -> p k t", k=KC)
        Wt = [A.take(KCh * 512).rearrange("p (k n) -> p k n", k=KCh) for _ in range(2)]
        ctx = setup() if setup is not None else None
        NTT = T // 128
        gi = 0
        KG = 8
        ngr = (KC + KG - 1) // KG
        seq = [(tg, cb, kh) for tg in range(S // T) for cb in range(NB) for kh in range(KH)]

        def issue_w(i):
            _, cb_, kh_ = seq[i]
            ws_ = i % 2
            dma("pool", Wt[ws_], W[kh_ * KCh * 128:(kh_ + 1) * KCh * 128, cb_ * 512:(cb_ + 1) * 512].rearrange("(k p) n -> p k n", p=128),
                [], [("W", ws_)], ("W", ws_))
        issue_w(0)
        cur_tg = -1
        for i, (tg, cb, kh) in enumerate(seq):
            if tg != cur_tg:
                cur_tg = tg
                for g in range(ngr):
                    k0, k1 = g * KG, min(KC, (g + 1) * KG)
                    dma("sp", inT[:, k0:k1, :],
                        inT_d[k0 * 128:k1 * 128, tg * T:(tg + 1) * T].rearrange("(k p) t -> p k t", p=128),
                        [(in_res, a, b) for a in range(4) for b in range(4)] if in_res == "hT_d" else [in_res],
                        [("inT", g)], ("inT", g))
            if i + 1 < len(seq):
                issue_w(i + 1)
            ws = i % 2
            for t in range(NTT):
                if KH == 1:
                    bank = (gi + t) % 8
                else:
                    bank = (cb % 2) * 4 + t
                for kc in range(KCh):
                    kk = kh * KCh + kc
                    mm(PS[bank][:, :], inT[:, kk, t * 128:(t + 1) * 128], Wt[ws][:, kc, :],
                       start=(kk == 0), stop=(kk == KC - 1),
                       reads=[("inT", kk // KG), ("W", ws)], writes=[("ps", bank)])
                if kh == KH - 1:
                    post(ctx, tg, cb, tg * NTT + t, bank)
            if KH == 1:
                gi += NTT
        phase_end()

    def make_resid_post(src):
        st = {"i": 0}

        def setup():
            xr = [A.take(1024, F32) for _ in range(4)]
            xo = [A.take(1024, F32) for _ in range(4)]
            return xr, xo

        def post(ctx, tg, cb, t, bank):
            xr, xo = ctx
            s = st["i"] % 4
            st["i"] += 1
            dma("sp", xr[s], src[t * 128:(t + 1) * 128, cb * 512:(cb + 1) * 512], [("x", t, cb)], [("xr", s)], ("xr", s))
            tt("dve", xo[s], PS[bank][:, :], xr[s], ALU.add, [("ps", bank), ("xr", s)], [("xo", s)])
            dma("sp", y[t * 128:(t + 1) * 128, cb * 512:(cb + 1) * 512], xo[s], [("xo", s)], [("x", t, cb)], ("xo", s))
        return setup, post

    def make_qkv_post(kinds, T):
        st = {"i": 0}
        NTT = T // 128
        gains = []
        for k in kinds:
            if k is not None and all(k is not g for g in gains):
                gains.append(k)

        def setup():
            cosT = A.take(NTT * 128 * 2, F32).rearrange("p (t d) -> p t d", t=NTT)
            sinT = A.take(NTT * 128 * 2, F32).rearrange("p (t d) -> p t d", t=NTT)
            gb = {}
            for gi_, g in enumerate(gains):
                gt = A.take(1024, F32)
                for r in range(4):
                    dma("sp", gt[:, r * 128:(r + 1) * 128], g.to_broadcast([128, 128]), [], [("gt", gi_)], ("c", 2 + gi_),
                        partial=True)
                gb[id(g)] = (gi_, gt)
            raw = [A.take(1024, F32) for _ in range(4)]
            sq = [A.take(1024, F32) for _ in range(4)]
            y1 = [A.take(1024, F32) for _ in range(4)]
            t1 = [A.take(1024, F32) for _ in range(4)]
            t2 = [A.take(1024, F32) for _ in range(4)]
            ob = [A.take(512) for _ in range(4)]
            sm = A.take(128, F32)
            return dict(cosT=cosT, sinT=sinT, gb=gb, raw=raw, sq=sq, y1=y1, t1=t1, t2=t2, ob=ob, sm=sm, tg=-1)

        def post(c, tg, cb, t, bank):
            if c["tg"] != tg:
                c["tg"] = tg
                dma("sp", c["cosT"], c_cos[tg * T:(tg + 1) * T, :].rearrange("(t p) d -> p t d", p=128), [], ["cosT"], ("c", 8))
                dma("sp", c["sinT"], c_sin[tg * T:(tg + 1) * T, :].rearrange("(t p) d -> p t d", p=128), [], ["sinT"], ("c", 9))
            i = st["i"]
            st["i"] += 1
            o = i % 4
            s = i % 4
            ti = t % NTT
            k = kinds[cb]
            ob = c["ob"][o]
            if k is None:
                cp("act", ob, PS[bank][:, :], [("ps", bank)], [("ob", o)])
            else:
                gi_, gt = c["gb"][id(k)]
                raw, sq, y1, t1, t2 = c["raw"][s], c["sq"][s], c["y1"][s], c["t1"][s], c["t2"][s]
                ss = c["sm"][:, 16 * s:16 * s + 4]
                sd = c["sm"][:, 16 * s + 4:16 * s + 8]
                rs = c["sm"][:, 16 * s + 8:16 * s + 12]
                cp("act", raw, PS[bank][:, :], [("ps", bank)], [("raw", s)])
                act(sq, PS[bank][:, :], AF.Square, [("ps", bank)], [("sq", s)])
                tred(ss, sq.rearrange("p (h d) -> p h d", h=4), [("sq", s)], [("ss", s)])
                act(sd, ss, AF.Sqrt, [("ss", s)], [("sd", s)], scale=1.0 / 128, bias=EPS)
                recip(rs, sd, [("sd", s)], [("rs", s)])
                tt("pool", y1, raw, gt, ALU.mult, [("raw", s), ("gt", gi_)], [("y1", s)])
                y1v = y1.rearrange("p (h d) -> p h d", h=4)
                t1v = t1.rearrange("p (h d) -> p h d", h=4)
                t2v = t2.rearrange("p (h d) -> p h d", h=4)
                cb_ = c["cosT"][:, ti:ti + 1, :].to_broadcast([128, 4, 128])
                tt("dve", t1v, y1v, cb_, ALU.mult, [("y1", s), "cosT"], [("t1", s)])
                sa = c["sinT"][:, ti:ti + 1, 0:64].to_broadcast([128, 4, 64])
                sb = c["sinT"][:, ti:ti + 1, 64:128].to_broadcast([128, 4, 64])
                tt("pool", t2v[:, :, 0:64], y1v[:, :, 64:128], sa, ALU.mult, [("y1", s), "sinT"], [("t2", s)])
                tt("pool", t2v[:, :, 64:128], y1v[:, :, 0:64], sb, ALU.mult, [("y1", s), "sinT"], [("t2", s)], )
                tt("dve", t1, t1, t2, ALU.add, [("t1", s), ("t2", s)], [("t1", s)])
                rb = rs.unsqueeze(2).to_broadcast([128, 4, 128])
                tt("dve", ob.rearrange("p (h d) -> p h d", h=4), t1v, rb, ALU.mult, [("t1", s), ("rs", s)], [("ob", o)])
            dma("sp", qkv_d[t * 128:(t + 1) * 128, cb * 512:(cb + 1) * 512], ob, [("ob", o)], [("qkv_d", cb, t)], ("ob", o))
        return setup, post

    def phase_ffn_in(l):
        T = 1024
        inT = A.take(32 * T).rearrange("p (k t) -> p k t", k=32)
        Wg = [A.take(32 * 256).rearrange("p (k n) -> p k n", k=32) for _ in range(2)]
        Wu = [A.take(32 * 256).rearrange("p (k n) -> p k n", k=32) for _ in range(2)]
        sg = [A.take(1024, F32) for _ in range(2)]
        ao = [A.take(1024) for _ in range(3)]
        wi = 0
        pi = 0
        ai = 0
        for tg in range(S // T):
            for g in range(4):
                dma("sp", inT[:, g * 8:(g + 1) * 8, :],
                    hT_d[g * 1024:(g + 1) * 1024, tg * T:(tg + 1) * T].rearrange("(k p) t -> p k t", p=128),
                    [("hT_d", a, b) for a in range(4) for b in range(4)], [("inT", g)], ("inT", g))
            for hb_ in range(HID // 256):
                ws = wi % 2
                wi += 1
                dma("pool", Wg[ws], w_gate[l, :, hb_ * 256:(hb_ + 1) * 256].rearrange("(k p) n -> p k n", p=128), [], [("Wg", ws)], ("Wg", ws))
                dma("pool", Wu[ws], w_up[l, :, hb_ * 256:(hb_ + 1) * 256].rearrange("(k p) n -> p k n", p=128), [], [("Wu", ws)], ("Wu", ws))
                for hc in range(2):
                    a = ai % 3
                    ai += 1
                    for th in range(2):
                        bg = (pi % 4) * 2
                        bu = bg + 1
                        pi += 1
                        for kc in range(32):
                            mm(PS[bg][:, :], Wg[ws][:, kc, hc * 128:(hc + 1) * 128], inT[:, kc, th * 512:(th + 1) * 512],
                               kc == 0, kc == 31, [("inT", kc // 8), ("Wg", ws)], [("ps", bg)])
                        for kc in range(32):
                            mm(PS[bu][:, :], Wu[ws][:, kc, hc * 128:(hc + 1) * 128], inT[:, kc, th * 512:(th + 1) * 512],
                               kc == 0, kc == 31, [("inT", kc // 8), ("Wu", ws)], [("ps", bu)])
                        s = pi % 2
                        act(sg[s], PS[bg][:, :], AF.Silu, [("ps", bg)], [("sg", s)])
                        tt("dve", ao[a][:, th * 512:(th + 1) * 512], PS[bu][:, :], sg[s], ALU.mult, [("ps", bu), ("sg", s)], [("ao", a)],)
                    r0 = hb_ * 256 + hc * 128
                    dma("sp", actT_d[r0:r0 + 128, tg * T:(tg + 1) * T], ao[a], [("ao", a)], [("actT_d", hb_, hc, tg)], ("ao", a))
        phase_end()

    def load_tok(dst, col0, width, res, key):
        return dma("sp", dst, qkv_d[:, col0:col0 + width].rearrange("(t p) c -> p t c", p=128), ["qkv_all"], [res], key)

    def transpose_tok(src, dstT, res_src, res_dst, evac_eng):
        for r in range(2):
            tp = psb(6)
            for k in range(8):
                tr(tp[:, k, :], src[:, r * 8 + k, :], [res_src, "ident"], [("ps", 6)])
            cp(evac_eng, dstT[:, r * 1024:(r + 1) * 1024], PS[6][:, :].bitcast(BF16), [("ps", 6)], [res_dst])

    def phase_attn_even(l):
        e = l // 2
        lam_init = 0.8 - 0.6 * math.exp(-0.3 * l)
        qtok = [A.take(2048).rearrange("p (t d) -> p t d", t=16) for _ in range(2)]
        ktok = [A.take(2048).rearrange("p (t d) -> p t d", t=16) for _ in range(2)]
        vA = [A.take(16 * 258).rearrange("p (t d) -> p t d", t=16) for _ in range(2)]
        vB = [A.take(16 * 130).rearrange("p (t d) -> p t d", t=16) for _ in range(2)]
        qT = [A.take(2048) for _ in range(2)]
        kT = [A.take(2048) for _ in range(2)]
        PT = [A.take(512) for _ in range(4)]
        mlong = A.take(3968)
        o1 = A.take(16 * 256 * 2, F32).rearrange("p (t d) -> p t d", t=16)
        res = [A.take(512, F32) for _ in range(4)]
        resb = [A.take(256) for _ in range(8)]
        osb = [A.take(520, F32) for _ in range(8)]
        sm2 = A.take(256, F32)
        sqb = [A.take(512, F32) for _ in range(4)]
        negh = A.take(16, F32)[:, 0:1]
        P.emit("pool", lambda en: en.memset(negh, -0.5), [], ["negh"])
        STB = [4, 5, 7]
        pending = []
        osi = 0
        junk = A.take(512, F32)
        catS = [A.take(2 * 2048).rearrange("p (c t) -> p c t", c=2) for _ in range(2)]
        lbc = A.take(1024, F32).rearrange("p (a d) -> p a d", a=4)
        sublng = A.take(512, F32)
        sm = A.take(64, F32)
        dma("sp", mlong, c_mlong[:, :], [], ["mlong"], ("c", 1))
        dma("sp", lbc, diff_lambda[e:e + 1, :].to_broadcast([128, 512]).rearrange("p (a d) -> p a d", a=4), [], ["lbc"], ("c", 2))
        dma("sp", sublng, diff_subln[e:e + 1, :].to_broadcast([128, 256]), [], ["sublng"], ("c", 3))
        for s in range(2):
            P.emit("pool", (lambda s_: (lambda en: en.memset(vA[s_][:, :, 256:257], 1.0)))(s), [], [("vA1", s)])
            P.emit("pool", (lambda s_: (lambda en: en.memset(vB[s_][:, :, 128:129], 1.0)))(s), [], [("vB1", s)])
        s1, s2, e1, e2, lam, neglam = [sm[:, i:i + 1] for i in range(6)]
        tt("dve", junk[:, 0:128], lbc[:, 0, :], lbc[:, 1, :], ALU.mult, ["lbc"], ["junk"])
        tred(s1, junk[:, 0:128], ["junk"], ["s1"])
        tt("dve", junk[:, 128:256], lbc[:, 2, :], lbc[:, 3, :], ALU.mult, ["lbc"], ["junk2"])
        tred(s2, junk[:, 128:256], ["junk2"], ["s2"])
        act(e1, s1, AF.Exp, ["s1"], ["e1"])
        act(e2, s2, AF.Exp, ["s2"], ["e2"])
        tt("dve", lam, e1, e2, ALU.subtract, ["e1", "e2"], ["lam"])
        ts("dve", neglam, lam, lam_init, ALU.add, ["lam"], ["neglam"], s2=-1.0, op1=ALU.mult)
        ts("dve", sublng, sublng, 1.0 - lam_init, ALU.mult, ["sublng"], ["sublng"])

        maps = []
        for h in range(8):
            for t in range(2):
                maps.append(("diff", h, t, h * 256 + t * 128, 2048 + h * 256 + t * 128, 4096 + h * 256, 256))
        for h in range(16):
            maps.append(("dil", h, 0, 6144 + h * 128, 8192 + h * 128, 10240 + h * 128, 128))
        pti = 0
        rsi = 0
        vai = 0
        vbi = 0
        csi = 0
        info = []
        for mi, (kind, h, t, qc, kc_, vc, vw) in enumerate(maps):
            if kind == "diff":
                if t == 0:
                    vai += 1
                info.append((vA[vai % 2], [("vA", vai % 2), ("vA1", vai % 2)], t == 0, ("vA", vai % 2)))
            else:
                vbi += 1
                info.append((vB[vbi % 2], [("vB", vbi % 2), ("vB1", vbi % 2)], True, ("vB", vbi % 2)))

        def prep_load(mi):
            kind, h, t, qc, kc_, vc, vw = maps[mi]
            s_ = mi % 2
            vext_, _, doload, vkey = info[mi]
            load_tok(qtok[s_], qc, 128, ("qtok", s_), ("qtok", s_))
            load_tok(ktok[s_], kc_, 128, ("ktok", s_), ("ktok", s_))
            if doload:
                dma("sp", vext_[:, :, 0:vw], qkv_d[:, vc:vc + vw].rearrange("(t p) c -> p t c", p=128), ["qkv_all"], [vkey], vkey)

        def prep_tr(mi):
            s_ = mi % 2
            transpose_tok(qtok[s_], qT[s_], ("qtok", s_), ("qT", s_), "dve")
            transpose_tok(ktok[s_], kT[s_], ("ktok", s_), ("kT", s_), "dve")
        prep_load(0)
        prep_tr(0)
        for mi, (kind, h, t, qc, kc_, vc, vw) in enumerate(maps):
            s = mi % 2
            vext, vres, _, _ = info[mi]
            if mi + 1 < len(maps):
                prep_load(mi + 1)
            if kind == "diff" and t == 0 or kind == "dil":
                csi += 1
            cs = csi % 2
            for qb in range(4):
                kts = list(range(16))
                if kind == "dil":
                    kts = [kt for kt in range(16) if (kt - (qb * 4 + 3)) <= 8 and ((qb * 4) - kt) <= 8]

                def st_mm(jj):
                    b = STB[jj % 3]
                    kt_ = kts[jj]
                    mm(PS[b][:, :], kT[s][:, kt_ * 128:(kt_ + 1) * 128], qT[s][:, qb * 512:(qb + 1) * 512], True, True,
                       [("kT", s), ("qT", s)], [("ps", b)])
                for jj in range(min(3, len(kts))):
                    st_mm(jj)
                for j, kt in enumerate(kts):
                    b = STB[j % 3]
                    p = pti % 4
                    pti += 1
                    act(PT[p], PS[b][:, :], AF.Exp, [("ps", b)], [("PT", p)], scale=SCALE)
                    if kind == "dil":
                        off = (qb * 4 - kt + 15) * 128
                        tt("dve", PT[p], PT[p], mlong[:, off:off + 512], ALU.mult, [("PT", p), "mlong"], [("PT", p)])
                    for qi in range(4):
                        mm(PS[qi][:, 0:vw + 1], PT[p][:, qi * 128:(qi + 1) * 128], vext[:, kt, 0:vw + 1],
                           j == 0, j == len(kts) - 1, [("PT", p)] + vres, [("ps", qi)])
                    if j + 3 < len(kts):
                        st_mm(j + 3)
                    if j == 3 and pending:
                        for f_ in pending:
                            f_()
                        del pending[:]
                for qi in range(4):
                    qt = qb * 4 + qi
                    k = osi % 8
                    osi += 1
                    ob_ = osb[k]
                    cp("dve", ob_[:, 0:vw + 1], PS[qi][:, 0:vw + 1], [("ps", qi)], [("osb", k)])
                    r = rsi % 8
                    rsi += 1
                    rz, rz2, ss, sd, rs = [sm2[:, 8 * r + q_:8 * r + q_ + 1] for q_ in range(5)]
                    recip(rz, ob_[:, vw:vw + 1], [("osb", k)], [("rz", r)])
                    if kind == "diff" and t == 0:
                        ts("dve", o1[:, qt, :], ob_[:, 0:256], rz, ALU.mult, [("osb", k), ("rz", r)], [("o1", qt)])
                        continue
                    if kind == "diff":
                        r4 = r % 4
                        tt("dve", rz2, rz, neglam, ALU.mult, [("rz", r), "neglam"], [("rz2", r)])
                        stt(res[r4], ob_[:, 0:256], rz2, o1[:, qt, :], ALU.mult, ALU.add, [("osb", k), ("rz2", r), ("o1", qt)], [("res", r4)])
                        sqj = sqb[r4]
                        tt("pool", sqj, res[r4], res[r4], ALU.mult, [("res", r4)], [("sqj", r4)])
                        tred(ss, sqj, [("sqj", r4)], [("ss", r)])
                        ts("pool", sd, ss, 1.0 / 256, ALU.mult, [("ss", r)], [("sd", r)], s2=EPS, op1=ALU.add)
                        tt("pool", rs, sd, negh, ALU.pow, [("sd", r), "negh"], [("rs", r)])
                        stt(resb[r], res[r4], rs, sublng, ALU.mult, ALU.mult, [("res", r4), ("rs", r), "sublng"], [("resb", r)])
                        nch = 2
                    else:
                        ts("dve", resb[r][:, 0:128], ob_[:, 0:128], rz, ALU.mult, [("osb", k), ("rz", r)], [("resb", r)])
                        nch = 1

                    def make_fin(r_, nch_, cs_, qt_):
                        def f_():
                            tp = psb(6)
                            for c in range(nch_):
                                tr(tp[:, c, :], resb[r_][:, c * 128:(c + 1) * 128], [("resb", r_), "ident"], [("ps", 6)])
                            cp("dve", catS[cs_][:, 0:nch_, qt_ * 128:(qt_ + 1) * 128], tp[:, 0:nch_, :], [("ps", 6)], [("catS", cs_)])
                        return f_
                    pending.append(make_fin(r, nch, cs, qt))
                if qb == 2 and mi + 1 < len(maps):
                    prep_tr(mi + 1)

            def make_store(kind_, h_, cs_):
                def f_():
                    if kind_ == "diff":
                        dma("sp", catT_d[h_ * 256:(h_ + 1) * 256, :].rearrange("(c p) t -> p c t", p=128), catS[cs_],
                            [("catS", cs_)], [("catT_d", h_)], ("catS", cs_))
                    else:
                        r0 = 2048 + h_ * 128
                        dma("sp", catT_d[r0:r0 + 128, :], catS[cs_][:, 0, :], [("catS", cs_)], [("catT_d", 8 + h_)], ("catS", cs_))
                return f_
            if (kind == "diff" and t == 1) or kind == "dil":
                pending.append(make_store(kind, h, cs))
        for f_ in pending:
            f_()
        del pending[:]
        phase_end()

    def phase_attn_odd(l):
        o = l // 2
        qtok = [A.take(16 * 512).rearrange("p (t d) -> p t d", t=16) for _ in range(2)]
        ktok = [A.take(2048).rearrange("p (t d) -> p t d", t=16) for _ in range(2)]
        vB = [A.take(16 * 130).rearrange("p (t d) -> p t d", t=16) for _ in range(2)]
        qT4 = [A.take(4 * 2048).rearrange("p (h t) -> p h t", h=4) for _ in range(2)]
        kT = [A.take(2048) for _ in range(2)]
        PT3 = [A.take(3 * 512).rearrange("p (j c) -> p j c", j=3) for _ in range(4)]
        tri = A.take(256)
        resb = [A.take(512).rearrange("p (h d) -> p h d", h=4) for _ in range(4)]
        osb = [A.take(2 * 520, F32).rearrange("p (b c) -> p b c", b=2) for _ in range(4)]
        STB = [4, 5, 7]
        catS = [A.take(4 * 2048).rearrange("p (h t) -> p h t", h=4) for _ in range(2)]
        sbc = A.take(64, F32)
        esink = A.take(64, F32)
        sm = A.take(64, F32)
        dma("sp", tri, c_tri[:, :], [], ["tri"], ("c", 1))
        dma("sp", sbc, win_sink[o:o + 1, :].to_broadcast([128, 32]), [], ["sbc"], ("c", 2))
        act(esink, sbc, AF.Exp, ["sbc"], ["esink"])
        for s in range(2):
            P.emit("pool", (lambda s_: (lambda en: en.memset(vB[s_][:, :, 128:129], 1.0)))(s), [], [("vB1", s)])
        rsi = 0
        for g in range(8):
            s = g % 2
            load_tok(qtok[s], g * 512, 512, ("qtok", s), ("qtok", s))
            load_tok(ktok[s], 4096 + g * 128, 128, ("ktok", s), ("ktok", s))
            dma("sp", vB[s][:, :, 0:128], qkv_d[:, 5120 + g * 128:5120 + (g + 1) * 128].rearrange("(t p) c -> p t c", p=128),
                ["qkv_all"], [("vB", s)], ("vB", s))
            for t in range(16):
                tp = psb(6)
                for hh in range(4):
                    tr(tp[:, hh, :], qtok[s][:, t, hh * 128:(hh + 1) * 128], [("qtok", s), "ident"], [("ps", 6)])
                cp("dve" if t % 2 == 0 else "act", qT4[s][:, :, t * 128:(t + 1) * 128], tp[:, 0:4, :], [("ps", 6)], [("qT4", s)])
            transpose_tok(ktok[s], kT[s], ("ktok", s), ("kT", s), "act")
            def stageA(qt):
                kts = [kt for kt in (qt - 1, qt, qt + 1) if 0 <= kt <= 15]
                pp = qt % 4
                for j, kt in enumerate(kts):
                    b = STB[j]
                    mm(PS[b][:, :], kT[s][:, kt * 128:(kt + 1) * 128], qT4[s][:, :, qt * 128:(qt + 1) * 128], True, True,
                       [("kT", s), ("qT4", s)], [("ps", b)])
                    act(PT3[pp][:, j, :], PS[b][:, :], AF.Exp, [("ps", b)], [("PT3", pp, j)], scale=SCALE)
                    if kt != qt:
                        m = tri[:, 0:128] if kt < qt else tri[:, 128:256]
                        pv = PT3[pp][:, j, :].rearrange("p (h q) -> p h q", h=4)
                        tt("dve", pv, pv, m.unsqueeze(1).to_broadcast([128, 4, 128]), ALU.mult, [("PT3", pp, j), "tri"], [("PT3", pp, j)])

            def stageB(qt):
                kts = [kt for kt in (qt - 1, qt, qt + 1) if 0 <= kt <= 15]
                pp = qt % 4
                r = qt % 4
                for hh in range(4):
                    bank = (qt % 2) * 2 + hh // 2
                    c0 = (hh % 2) * 130
                    for j, kt in enumerate(kts):
                        mm(PS[bank][:, c0:c0 + 129], PT3[pp][:, j, hh * 128:(hh + 1) * 128], vB[s][:, kt, 0:129],
                           j == 0, j == len(kts) - 1, [("PT3", pp, j), ("vB", s), ("vB1", s)], [("ps", bank)])
                for bi in range(2):
                    bank = (qt % 2) * 2 + bi
                    cp("dve" if bi == 0 else "act", osb[r][:, bi, 0:259], PS[bank][:, 0:259], [("ps", bank)], [("osb", r, bi)])
                for hh in range(4):
                    bi = hh // 2
                    c0 = (hh % 2) * 130
                    den = sm[:, 8 * r + hh:8 * r + hh + 1]
                    rz = sm[:, 8 * r + 4 + hh:8 * r + 5 + hh]
                    hd = g * 4 + hh
                    tt("dve", den, osb[r][:, bi, c0 + 128:c0 + 129], esink[:, hd:hd + 1], ALU.add, [("osb", r, bi), "esink"], [("den", r, hh)])
                    recip(rz, den, [("den", r, hh)], [("rz", r, hh)])
                    ts("dve", resb[r][:, hh, :], osb[r][:, bi, c0:c0 + 128], rz, ALU.mult, [("osb", r, bi), ("rz", r, hh)], [("resb", r)])

            def stageC(qt):
                r = qt % 4
                tp = psb(6)
                for hh in range(4):
                    tr(tp[:, hh, :], resb[r][:, hh, :], [("resb", r), "ident"], [("ps", 6)])
                cp("act", catS[s][:, :, qt * 128:(qt + 1) * 128], tp[:, 0:4, :], [("ps", 6)], [("catS", s)])

            stageA(0)
            for qt in range(16):
                if qt + 1 < 16:
                    stageA(qt + 1)
                stageB(qt)
                if qt >= 1:
                    stageC(qt - 1)
            stageC(15)
            dma("sp", catT_d[g * 512:(g + 1) * 512, :].rearrange("(h p) t -> p h t", p=128), catS[s], [("catS", s)], [("catT_d", g)], ("catS", s))
        phase_end()

    phases = []
    for l in range(n_layers):
        phases += [(l, "norm1"), (l, "mixin"), (l, "attn"), (l, "mixout"), (l, "norm2"), (l, "ffnin"), (l, "ffnout")]
    xsrc = x_in
    if start_at is not None:
        phases = phases[phases.index(start_at):]
    for (l, ph) in phases:
        if ph == "norm1":
            phase_norm(xsrc, mix_norm[l, :])
        elif ph == "mixin":
            if l % 2 == 0:
                e = l // 2
                gq, gk = diff_q_norm[e:e + 1, :], diff_k_norm[e:e + 1, :]
                hq, hk = dil_q_norm[e:e + 1, :], dil_k_norm[e:e + 1, :]
                kinds = [gq] * 4 + [gk] * 4 + [None] * 4 + [hq] * 4 + [hk] * 4 + [None] * 4
                setup, post = make_qkv_post(kinds, 1024)
                phase_proj_tok(hT_d, 32, 1024, hy_w_in[e, :, :], 24, 1, post, setup)
            else:
                o = l // 2
                gq, gk = win_q_norm[o:o + 1, :], win_k_norm[o:o + 1, :]
                kinds = [gq] * 8 + [gk] * 2 + [None] * 2
                setup, post = make_qkv_post(kinds, 1024)
                phase_proj_tok(hT_d, 32, 1024, win_w_in[o, :, :], 12, 1, post, setup)
        elif ph == "attn":
            if l % 2 == 0:
                phase_attn_even(l)
            else:
                phase_attn_odd(l)
        elif ph == "mixout":
            setup, post = make_resid_post(xsrc)
            W = hy_w_out[l // 2, :, :] if l % 2 == 0 else win_w_out[l // 2, :, :]
            phase_proj_tok(catT_d, 32, 1024, W, 8, 1, post, setup, in_res="catT_all")
            xsrc = y
        elif ph == "norm2":
            phase_norm(xsrc, ffn_norm[l, :])
        elif ph == "ffnin":
            phase_ffn_in(l)
        elif ph == "ffnout":
            HK = HID // 2
            setup, post = make_resid_post(xsrc)
            phase_proj_tok(actT_d[0:HK, :], 43, 1024, w_down[l, 0:HK, :], 8, 1, post, setup, in_res="actT_all")
            setup, post = make_resid_post(y)
            phase_proj_tok(actT_d[HK:HID, :], 43, 1024, w_down[l, HK:HID, :], 8, 1, post, setup, in_res="actT_all")
        if stop_after == (l, ph):
            break

    P.finalize()
    with nc.Block() as block:
        @block.tensor
        def _(e):
            P.replay("pe", e)

        @block.scalar
        def _(e):
            P.replay("act", e)

        @block.vector
        def _(e):
            P.replay("dve", e)

        @block.gpsimd
        def _(e):
            P.replay("pool", e)

        @block.sync
        def _(e):
            P.replay("sp", e)
    return nc


def host_consts():
    bf = ml_dtypes.bfloat16
    ident = np.eye(128, dtype=np.float32).astype(bf)
    inv_freq = (10000.0 ** (-np.arange(0, 128, 2, dtype=np.float32) / 128)).astype(np.float32)
    ang = np.arange(S, dtype=np.float32)[:, None] * inv_freq[None, :]
    ang = np.concatenate([ang, ang], axis=-1)
    cos = np.cos(ang).astype(np.float32)
    sin = np.sin(ang).astype(np.float32)
    sin[:, :64] *= -1.0
    kp = np.arange(128)[:, None]
    u = np.arange(3968)[None, :]
    d = kp - u + 1920
    ad = np.abs(d)
    cnt = (ad <= 64).astype(np.float32) + ((d % 4 == 0) & (ad <= 256)) + ((d % 16 == 0) & (ad <= 1024))
    mlong = cnt.astype(bf)
    qp = np.arange(128)[None, :]
    tri = np.concatenate([(kp >= qp), (kp <= qp)], axis=1).astype(np.float32).astype(bf)
    return dict(c_ident=ident, c_cos=cos, c_sin=sin, c_mlong=mlong, c_tri=tri)


_CACHE = {}


def kernel(**inputs):
    n = 8
    if "nc" not in _CACHE:
        _CACHE["nc"] = build()
    nc = _CACHE["nc"]
    consts = host_consts()
    x = np.asarray(inputs["x"], dtype=np.float32)
    shared = {k: np.asarray(v) for k, v in inputs.items() if k != "x"}
    shared["diff_lambda"] = shared["diff_lambda"].reshape(2, 512)
    in_maps = []
    for c in range(n):
        m = {"x": x[c]}
        m.update(shared)
        m.update(consts)
        in_maps.append(m)
    res = run_bass_kernel_spmd(nc, in_maps, core_ids=list(range(n)))
    return np.stack([r["y"] for r in res.results], axis=0).astype(np.float32)
```
